# Optimizing a Trainium2 kernel written in Bass

```python
import jax, jax.numpy as jnp
from jax import lax
import numpy as np

D_MODEL = 2048
BATCH = 16
SEQ = 256
DEPTH = 2
DEC_BATCH = 8
DEC_SEQ = 4096
PAST_LEN = 256

GRID_W = 64
N_EVEN = (DEPTH + 1) // 2
N_ODD = DEPTH // 2
ALPHA = (2 * DEPTH) ** 0.25
LN_EPS = 1e-5
RMS_EPS = 1e-6
Q_BLOCK = 128
N_MOD = 9
D_FF = 5632
NA_HEADS = 8
NA_HEAD_DIM = 128
NA_WIDTH = NA_HEADS * NA_HEAD_DIM
NA_ROWS = 8
NA_COLS = 16
POOL_WINDOWS = (2, 4, 8, 16)
POOL_GROUPS = 4
POOL_CH = D_MODEL // 8
POOL_WIDTH = POOL_GROUPS * POOL_CH
MIX0_IN = 3 * NA_WIDTH + POOL_WIDTH
MIX0_OUT = NA_WIDTH + POOL_WIDTH
MLA_HEADS = 16
Q_LORA = 512
KV_LORA = 512
QK_NOPE = 128
QK_ROPE = 64
V_DIM = 128
MLA_DOWN = Q_LORA + KV_LORA + QK_ROPE
ROPE_AXIS = QK_ROPE // 2
ROPE_BASE = 10000.0

kernel_name = 'hybrid_diffusion_na_pool_mla_step'


def layer_norm(x, g, b):
    xf = x.astype(jnp.float32)
    mu = jnp.mean(xf, axis=-1, keepdims=True)
    var = jnp.mean(jnp.square(xf - mu), axis=-1, keepdims=True)
    y = (xf - mu) * lax.rsqrt(var + LN_EPS) * g.astype(jnp.float32) + b.astype(jnp.float32)
    return y.astype(x.dtype)


def rms_norm(x, g):
    xf = x.astype(jnp.float32)
    y = xf * lax.rsqrt(jnp.mean(jnp.square(xf), axis=-1, keepdims=True) + RMS_EPS) * g.astype(jnp.float32)
    return y.astype(x.dtype)


def swiglu(h, w1, w3, w2):
    return (jax.nn.silu(h @ w1) * (h @ w3)) @ w2


def modulate(x, mod, j):
    shift = mod[:, None, 3 * j]
    scale = mod[:, None, 3 * j + 1]
    gate = mod[:, None, 3 * j + 2]
    return x * (1 + scale) + shift, gate


def post_norm(x, update, g, b):
    return layer_norm(ALPHA * x + update, g, b)


def dense_attention(q, k, v, scale):
    B_, Lq, H, Dq = q.shape
    nb = Lq // Q_BLOCK
    qb = q.reshape(B_, nb, Q_BLOCK, H, Dq).transpose(1, 0, 2, 3, 4)

    def block(qi):
        s = jnp.einsum('bqhd,bkhd->bhqk', qi, k).astype(jnp.float32) * scale
        p = jax.nn.softmax(s, axis=-1).astype(v.dtype)
        return jnp.einsum('bhqk,bkhd->bqhd', p, v)

    o = lax.map(block, qb)
    return o.transpose(1, 0, 2, 3, 4).reshape(B_, Lq, H, v.shape[-1])


def axial_rope_angles(n_tokens):
    t = jnp.arange(n_tokens)
    inv = ROPE_BASE ** (-jnp.arange(0, ROPE_AXIS, 2, dtype=jnp.float32) / ROPE_AXIS)
    ang_row = (t // GRID_W).astype(jnp.float32)[:, None] * inv[None, :]
    ang_col = (t % GRID_W).astype(jnp.float32)[:, None] * inv[None, :]
    return ang_row, ang_col


def _rotate(x, ang):
    x1, x2 = jnp.split(x, 2, axis=-1)
    cos, sin = jnp.cos(ang), jnp.sin(ang)
    return jnp.concatenate([x1 * cos - x2 * sin, x2 * cos + x1 * sin], axis=-1)


def axial_rope(x, ang_row, ang_col):
    bshape = (x.shape[1],) + (1,) * (x.ndim - 3) + (ang_row.shape[-1],)
    xr, xc = jnp.split(x.astype(jnp.float32), 2, axis=-1)
    out = jnp.concatenate([_rotate(xr, ang_row.reshape(bshape)),
                           _rotate(xc, ang_col.reshape(bshape))], axis=-1)
    return out.astype(x.dtype)


def multiscale_pool(u, w_pool, pool_scale):
    B_, L, _ = u.shape
    ug = u.reshape(B_, L, POOL_GROUPS, POOL_CH)
    cs = jnp.cumsum(ug.astype(jnp.float32), axis=1)
    cs = jnp.concatenate([jnp.zeros_like(cs[:, :1]), cs], axis=1)
    t = jnp.arange(L)
    means = []
    for g, w in enumerate(POOL_WINDOWS):
        lo = jnp.clip(t - w // 2, 0, L)
        hi = jnp.clip(t + w // 2, 0, L)
        csg = cs[:, :, g]
        s = jnp.take(csg, hi, axis=1) - jnp.take(csg, lo, axis=1)
        means.append(s / (hi - lo).astype(jnp.float32)[None, :, None])
    pooled = jnp.stack(means, axis=2)
    d = (pooled - ug.astype(jnp.float32)).astype(u.dtype)
    y = jnp.einsum('blgc,gcd->blgd', d, w_pool).reshape(B_, L, POOL_WIDTH)
    return y * pool_scale


def neighbourhood_attention(q, k, v, k_ctx, v_ctx, rpb):
    B_, L, H, Dh = q.shape
    rows = L // GRID_W
    kr = min(NA_ROWS, rows)
    kc = NA_COLS
    scale = Dh ** -0.5
    col = jnp.arange(GRID_W)
    col_start = jnp.clip(col - kc // 2, 0, GRID_W - kc)
    col_idx = col_start[:, None] + jnp.arange(kc)[None, :]
    dc = col_idx - col[:, None]
    kg = k.reshape(B_, rows, GRID_W, H, Dh)
    vg = v.reshape(B_, rows, GRID_W, H, Dh)
    q_rows = q.reshape(B_, rows, GRID_W, H, Dh).transpose(1, 0, 2, 3, 4)

    def row_block(args):
        r, qb = args
        rs = jnp.clip(r - kr // 2, 0, rows - kr)
        kb = lax.dynamic_slice_in_dim(kg, rs, kr, axis=1)
        vb = lax.dynamic_slice_in_dim(vg, rs, kr, axis=1)
        kq = kb[:, :, col_idx]
        vq = vb[:, :, col_idx]
        dr = rs + jnp.arange(kr) - r
        bias = rpb[:, dr[:, None, None] + NA_ROWS - 1, dc[None] + NA_COLS - 1]
        bias = bias.transpose(0, 2, 1, 3).reshape(H, GRID_W, kr * kc).astype(jnp.float32)
        s_loc = jnp.einsum('bqhd,biqjhd->bhqij', qb, kq).astype(jnp.float32)
        s_loc = s_loc.reshape(B_, H, GRID_W, kr * kc) * scale + bias[None]
        s_ctx = jnp.einsum('bqhd,bkhd->bhqk', qb, k_ctx).astype(jnp.float32) * scale
        p = jax.nn.softmax(jnp.concatenate([s_loc, s_ctx], axis=-1), axis=-1).astype(v.dtype)
        p_loc = p[..., :kr * kc].reshape(B_, H, GRID_W, kr, kc)
        p_ctx = p[..., kr * kc:]
        return (jnp.einsum('bhqij,biqjhd->bqhd', p_loc, vq)
                + jnp.einsum('bhqk,bkhd->bqhd', p_ctx, v_ctx))

    o = lax.map(row_block, (jnp.arange(rows), q_rows))
    return o.transpose(1, 0, 2, 3, 4).reshape(B_, L, H, Dh)


def even_mixer(h, ctx, w_in, w_out, rpb, w_pool, pool_scale):
    B_, L, _ = h.shape
    proj = h @ w_in
    q = proj[..., :NA_WIDTH].reshape(B_, L, NA_HEADS, NA_HEAD_DIM)
    k = proj[..., NA_WIDTH:2 * NA_WIDTH].reshape(B_, L, NA_HEADS, NA_HEAD_DIM)
    v = proj[..., 2 * NA_WIDTH:3 * NA_WIDTH].reshape(B_, L, NA_HEADS, NA_HEAD_DIM)
    u = proj[..., 3 * NA_WIDTH:]
    if ctx is None:
        a = dense_attention(q, k, v, NA_HEAD_DIM ** -0.5)
        st = (k, v)
    else:
        a = neighbourhood_attention(q, k, v, ctx[0], ctx[1], rpb)
        st = None
    pooled = multiscale_pool(u, w_pool, pool_scale)
    y = jnp.concatenate([a.reshape(B_, L, NA_WIDTH), pooled], axis=-1) @ w_out
    return y, st


def mla_expand(ckv, kpe, w_ukv):
    B_, L, _ = ckv.shape
    kv = (ckv @ w_ukv).reshape(B_, L, MLA_HEADS, QK_NOPE + V_DIM)
    k_pe = jnp.broadcast_to(kpe[:, :, None, :], (B_, L, MLA_HEADS, QK_ROPE))
    k = jnp.concatenate([kv[..., :QK_NOPE], k_pe], axis=-1)
    return k, kv[..., QK_NOPE:]


def odd_mixer(h, ctx, w_down, q_norm, w_uq, kv_norm, w_ukv, w_out):
    B_, L, _ = h.shape
    down = h @ w_down
    cq = rms_norm(down[..., :Q_LORA], q_norm)
    ckv = rms_norm(down[..., Q_LORA:Q_LORA + KV_LORA], kv_norm)
    kpe = down[..., Q_LORA + KV_LORA:]
    q = (cq @ w_uq).reshape(B_, L, MLA_HEADS, QK_NOPE + QK_ROPE)
    if ctx is None:
        k, v = mla_expand(ckv, kpe, w_ukv)
        st = (ckv, kpe)
    else:
        ang_r, ang_c = axial_rope_angles(L)
        q = jnp.concatenate([q[..., :QK_NOPE], axial_rope(q[..., QK_NOPE:], ang_r, ang_c)], axis=-1)
        k_lat, v_lat = mla_expand(ckv, axial_rope(kpe, ang_r, ang_c), w_ukv)
        k_ctx, v_ctx = mla_expand(ctx[0], ctx[1], w_ukv)
        k = jnp.concatenate([k_lat, k_ctx], axis=1)
        v = jnp.concatenate([v_lat, v_ctx], axis=1)
        st = None
    o = dense_attention(q, k, v, (QK_NOPE + QK_ROPE) ** -0.5)
    return o.reshape(B_, L, MLA_HEADS * V_DIM) @ w_out, st


def run_trunk(x, cond, caches, w_mod, b_mod, ln_g, ln_b, ffn_w1, ffn_w3, ffn_w2,
              na_w_in, mix0_w_out, na_rpb, pool_w, pool_scale,
              mla_w_down, mla_q_norm, mla_w_uq, mla_kv_norm, mla_w_ukv, mla_w_out):
    ctx_even, ctx_odd = [], []
    for l in range(DEPTH):
        mod = (jax.nn.silu(cond) @ w_mod[l] + b_mod[l]).reshape(cond.shape[0], N_MOD, D_MODEL)
        h, gate = modulate(x, mod, 0)
        x = post_norm(x, 0.5 * gate * swiglu(h, ffn_w1[l, 0], ffn_w3[l, 0], ffn_w2[l, 0]),
                      ln_g[l, 0], ln_b[l, 0])
        h, gate = modulate(x, mod, 1)
        i = l // 2
        if l % 2 == 0:
            ctx = None if caches is None else (caches[0][:, i], caches[1][:, i])
            y, st = even_mixer(h, ctx, na_w_in[i], mix0_w_out[i], na_rpb[i], pool_w[i], pool_scale[i])
            ctx_even.append(st)
        else:
            ctx = None if caches is None else (caches[2][:, i], caches[3][:, i])
            y, st = odd_mixer(h, ctx, mla_w_down[i], mla_q_norm[i], mla_w_uq[i],
                              mla_kv_norm[i], mla_w_ukv[i], mla_w_out[i])
            ctx_odd.append(st)
        x = post_norm(x, gate * y, ln_g[l, 1], ln_b[l, 1])
        h, gate = modulate(x, mod, 2)
        x = post_norm(x, 0.5 * gate * swiglu(h, ffn_w1[l, 1], ffn_w3[l, 1], ffn_w2[l, 1]),
                      ln_g[l, 2], ln_b[l, 2])
    return x, ctx_even, ctx_odd


def setup_inputs(seed: int = 0) -> dict:
    key = jax.random.key(seed)
    ks = iter(jax.random.split(key, 40))

    def nrm(shape, scale):
        return jax.random.normal(next(ks), shape, jnp.float32) * scale

    beta = (8 * DEPTH) ** -0.25
    D = D_MODEL
    return {
        'x_prompt': nrm((BATCH, SEQ, D), 1.0),
        'x_sample': nrm((DEC_BATCH, DEC_SEQ, D), 1.0),
        'cache_na_k': nrm((DEC_BATCH, N_EVEN, PAST_LEN, NA_HEADS, NA_HEAD_DIM), 1.0),
        'cache_na_v': nrm((DEC_BATCH, N_EVEN, PAST_LEN, NA_HEADS, NA_HEAD_DIM), 1.0),
        'cache_mla_ckv': nrm((DEC_BATCH, N_ODD, PAST_LEN, KV_LORA), 1.0),
        'cache_mla_kpe': nrm((DEC_BATCH, N_ODD, PAST_LEN, QK_ROPE), 1.0),
        'c': nrm((DEC_BATCH, D), 1.0),
        'c_ctx': nrm((D,), 1.0),
        'w_mod': nrm((DEPTH, D, N_MOD * D), 0.5 * D ** -0.5),
        'b_mod': nrm((DEPTH, N_MOD * D), 0.02),
        'ln_g': 1.0 + nrm((DEPTH, 3, D), 0.02),
        'ln_b': nrm((DEPTH, 3, D), 0.02),
        'ffn_w1': nrm((DEPTH, 2, D, D_FF), D ** -0.5),
        'ffn_w3': nrm((DEPTH, 2, D, D_FF), D ** -0.5),
        'ffn_w2': nrm((DEPTH, 2, D_FF, D), beta * D_FF ** -0.5),
        'na_w_in': nrm((N_EVEN, D, MIX0_IN), D ** -0.5),
        'mix0_w_out': nrm((N_EVEN, MIX0_OUT, D), beta * MIX0_OUT ** -0.5),
        'na_rpb': nrm((N_EVEN, NA_HEADS, 2 * NA_ROWS - 1, 2 * NA_COLS - 1), 0.1),
        'pool_w': nrm((N_EVEN, POOL_GROUPS, POOL_CH, POOL_CH), POOL_CH ** -0.5),
        'pool_scale': 1.0 + nrm((N_EVEN, POOL_WIDTH), 0.02),
        'mla_w_down': nrm((N_ODD, D, MLA_DOWN), D ** -0.5),
        'mla_q_norm': 1.0 + nrm((N_ODD, Q_LORA), 0.02),
        'mla_w_uq': nrm((N_ODD, Q_LORA, MLA_HEADS * (QK_NOPE + QK_ROPE)), Q_LORA ** -0.5),
        'mla_kv_norm': 1.0 + nrm((N_ODD, KV_LORA), 0.02),
        'mla_w_ukv': nrm((N_ODD, KV_LORA, MLA_HEADS * (QK_NOPE + V_DIM)), KV_LORA ** -0.5),
        'mla_w_out': nrm((N_ODD, MLA_HEADS * V_DIM, D), beta * (MLA_HEADS * V_DIM) ** -0.5),
    }


def reference(x_prompt, x_sample, cache_na_k, cache_na_v, cache_mla_ckv, cache_mla_kpe, c, c_ctx,
              w_mod, b_mod, ln_g, ln_b, ffn_w1, ffn_w3, ffn_w2,
              na_w_in, mix0_w_out, na_rpb, pool_w, pool_scale,
              mla_w_down, mla_q_norm, mla_w_uq, mla_kv_norm, mla_w_ukv, mla_w_out):
    y_prompt, ctx_even, ctx_odd = run_trunk(
        x_prompt, c_ctx[None, :], None, w_mod, b_mod, ln_g, ln_b, ffn_w1, ffn_w3, ffn_w2,
        na_w_in, mix0_w_out, na_rpb, pool_w, pool_scale,
        mla_w_down, mla_q_norm, mla_w_uq, mla_kv_norm, mla_w_ukv, mla_w_out)
    y_sample, _, _ = run_trunk(
        x_sample, c, (cache_na_k, cache_na_v, cache_mla_ckv, cache_mla_kpe),
        w_mod, b_mod, ln_g, ln_b, ffn_w1, ffn_w3, ffn_w2,
        na_w_in, mix0_w_out, na_rpb, pool_w, pool_scale,
        mla_w_down, mla_q_norm, mla_w_uq, mla_kv_norm, mla_w_ukv, mla_w_out)
    new_na_k = jnp.stack([s[0] for s in ctx_even], axis=1)
    new_na_v = jnp.stack([s[1] for s in ctx_even], axis=1)
    new_mla_ckv = jnp.stack([s[0] for s in ctx_odd], axis=1)
    new_mla_kpe = jnp.stack([s[1] for s in ctx_odd], axis=1)
    return (y_prompt, y_sample, new_na_k, new_na_v, new_mla_ckv, new_mla_kpe)
```

```python
import contextlib
import numpy as np
import concourse.bass as bass
import concourse.mybir as mybir
from concourse.bass_utils import run_bass_kernel_spmd

F32 = mybir.dt.float32
BF16 = mybir.dt.bfloat16
AF = mybir.ActivationFunctionType
ALU = mybir.AluOpType

D = 2048
DC = 16
NPR = 512
NSA = 4096
NT = NPR + NSA
TS = 512
NTILE = NT // TS
DFF = 5632
FC = DFF // 128
DEPTH = 2
ALPHA = (2 * DEPTH) ** 0.25
LN_EPS = 1e-5
RMS_EPS = 1e-6
EPS_P = LN_EPS / (ALPHA * ALPHA)
GRID_W = 64
NKEY_MLA = NSA + 256
UPAD = 8
UTLEN = (256 + 2 * UPAD) * 2 + (NSA + 2 * UPAD)


IN_SPECS = {
    "xp": ("xp", [NPR, D]), "xs": ("xs", [NSA, D]), "cnak": ("cnak", [256, 1024]), "cnav": ("cnav", [256, 1024]),
    "cckv": ("cckv", [256, 512]), "ckpe": ("ckpe", [256, 64]), "cond": ("cond", [2, D]),
    "w_mod": ("w_mod", [DEPTH, D, 9 * D]), "b_mod": ("b_mod", [DEPTH, 9 * D]),
    "ln_g": ("ln_g", [DEPTH, 3, D]), "ln_b": ("ln_b", [DEPTH, 3, D]),
    "ffn_w1": ("ffn_w1", [DEPTH, 2, D, DFF]), "ffn_w3": ("ffn_w3", [DEPTH, 2, D, DFF]),
    "ffn_w2": ("ffn_w2", [DEPTH, 2, DFF, D]), "na_w_in": ("na_w_in", [1, D, 4096]),
    "mix0_w_out": ("mix0_w_out", [1, D, D]), "pool_w": ("pool_w", [1, 4, 256, 256]),
    "pool_scale": ("pool_scale", [1, 1024]), "mla_w_down": ("mla_w_down", [1, D, 1088]),
    "mla_q_norm": ("mla_q_norm", [1, 512]), "mla_w_uq": ("mla_w_uq", [1, 512, 3072]),
    "mla_kv_norm": ("mla_kv_norm", [1, 512]), "mla_w_ukv": ("mla_w_ukv", [1, 512, 4096]),
    "mla_w_out": ("mla_w_out", [1, D, D]),
    "identd": ("ident", [128, 128]), "rpbrep": ("rpbrep", [8 * 31 * 64 * 127]),
    "emask": ("emask", [3, 8, 128, 512]), "invcnt": ("invcnt", [4, 128, UTLEN]),
    "ropec": ("ropec", [64, NSA]), "ropes": ("ropes", [64, NSA]),
}


class Eng:
    def __init__(self, kb, eng, name):
        self.kb = kb
        self.e = eng
        self.name = name
        self.sem = kb.newsem("es_" + name)
        self.cnt = 0
        self.seen = {}

    def wait(self, *toks):
        for t in toks:
            if t is None:
                continue
            if isinstance(t, list):
                self.wait(*t)
                continue
            o, c = t
            if o is self:
                continue
            if self.seen.get(o, 0) >= c:
                continue
            self.e.wait_ge(o.sem, c)
            self.seen[o] = c

    def sig(self, inst):
        self.cnt += 1
        inst.then_inc(self.sem, 1)
        return (self, self.cnt)

    def chain(self, inst):
        t = self.sig(inst)
        self.e.wait_ge(self.sem, self.cnt)
        return t


class DSem:
    def __init__(self, kb, name):
        self.sem = kb.newsem("ds_" + name)
        self.cnt = 0


class KB:
    def __init__(self, dbg=(), plan=None):
        self.dbg = set(dbg)
        self.plan = plan
        nc = bass.Bass("TRN2", target_bir_lowering=False)
        self.nc = nc
        self._uid = 0
        _orig = nc.sbuf_tensor

        def _sbuf_unique(name, shape, dt):
            self._uid += 1
            return _orig(f"{name}_{self._uid}", shape, dt)
        self._sbuf = _sbuf_unique
        self.es = contextlib.ExitStack()
        self.dsems = {}
        self.outs = []
        self.ins = []
        self.pool_gate = None
        self.ds_pool = []
        self.ds_map = {}

    def __getattr__(self, name):
        if name in IN_SPECS:
            nm, shape = IN_SPECS[name]
            ap = self.nc.dram_tensor(nm, list(shape), F32, kind="ExternalInput").ap()
            self.__dict__[name] = ap
            self.ins.append(nm)
            return ap
        raise AttributeError(name)

    def newsem(self, name):
        return self.es.enter_context(self.nc.semaphore(name))

    def ds(self, name):
        if name.startswith("cv_") or name.startswith("sw"):
            if name not in self.dsems:
                self.dsems[name] = DSem(self, name)
            return self.dsems[name]
        if name not in self.ds_map:
            i = len(self.ds_map)
            if i >= len(self.ds_pool):
                d = DSem(self, f"pool{i}")
                self.ds_pool.append(d)
                self.dsems[f"pool{i}"] = d
            self.ds_map[name] = self.ds_pool[i]
        return self.ds_map[name]

    def dma(self, q, out, in_, ds, nogate=False):
        if q is self.POOL and self.pool_gate is not None and not nogate:
            q.wait(self.pool_gate)
        q.e.dma_start(out=out, in_=in_).then_inc(ds.sem, 16)
        ds.cnt += 16
        return (ds, ds.cnt)

    def dram_in(self, name, shape, dt=F32):
        return self.nc.dram_tensor(name, list(shape), dt, kind="ExternalInput").ap()

    def dram_out(self, name, shape, dt=F32):
        self.outs.append(name)
        return self.nc.dram_tensor(name, list(shape), dt, kind="ExternalOutput").ap()

    def scratch(self, name, shape, dt):
        if name in self.dbg:
            self.outs.append(name)
            return self.nc.dram_tensor(name, list(shape), dt, kind="ExternalOutput").ap()
        return self.nc.dram_tensor(name, list(shape), dt, kind="Internal").ap()

    def barrier(self):
        nc = self.nc
        for nm, d in self.dsems.items():
            if d.cnt > 0 and not nm.startswith("cv_"):
                self.SP.wait((d, d.cnt))
        toks = [(e, e.cnt) for e in (self.PE, self.ACT, self.DVE) if e.cnt > 0]
        self.SP.cnt += 1
        self.SP.e.sem_inc(self.SP.sem, 1)
        toks.append((self.SP, self.SP.cnt))
        for e in (self.PE, self.ACT, self.DVE, self.SP):
            e.wait(toks)
        self.pool_gate = toks
        self.ds_map = {}
        for e in (self.PE, self.ACT, self.DVE, self.SP):
            for nm, d in self.dsems.items():
                if not nm.startswith("cv_"):
                    e.seen[d] = d.cnt

    def build(self):
        nc = self.nc
        with self.es:
            self._build()
        return nc

    def _build(self):
        nc = self.nc
        es = self.es
        self.PE = Eng(self, nc.tensor, "pe")
        self.ACT = Eng(self, nc.scalar, "act")
        self.DVE = Eng(self, nc.vector, "dve")
        self.POOL = Eng(self, nc.gpsimd, "pool")
        self.SP = Eng(self, nc.sync, "sp")
        self.yp = self.dram_out("yp", [NPR, D])
        self.ys = self.dram_out("ys", [NSA, D])
        self.onk = self.dram_out("onk", [NPR, 1024])
        self.onv = self.dram_out("onv", [NPR, 1024])
        self.onckv = self.dram_out("onckv", [NPR, 512])
        self.onkpe = self.dram_out("onkpe", [NPR, 64])
        sc = self.scratch
        self.XT = sc("XT", [DC, 128, NT], F32)
        self.HT = sc("HT", [DC, 128, NT], BF16)
        self.W1B = [[sc(f"W1B{l}{j}", [22, 128, 16, 256], BF16) for j in range(2)] for l in range(DEPTH)]
        self.W3B = [[sc(f"W3B{l}{j}", [22, 128, 16, 256], BF16) for j in range(2)] for l in range(DEPTH)]
        self.W2B = [[sc(f"W2B{l}{j}", [DC, 128, FC, 128], BF16) for j in range(2)] for l in range(DEPTH)]
        self.WINB = sc("WINB", [24, 128, 16, 128], BF16)
        self.WO0B = sc("WO0B", [DC, 128, 16, 128], BF16)
        self.WO1B = sc("WO1B", [DC, 128, 16, 128], BF16)
        self.QT = sc("QT", [8, 128, NT], BF16)
        self.KT = sc("KT", [8, 128, NT], BF16)
        self.VS = sc("VS", [NT, 1024], BF16)
        self.UT = sc("UT", [8, 128, UTLEN], F32)
        self.CT = sc("CT", [DC, 128, NT], BF16)
        self.EALL = sc("EALL", [3, 8, 8, 128, 512], BF16)
        self.CQT = sc("CQT", [4, 128, NT], BF16)
        self.CKVT = sc("CKVT", [4, 128, NT + 256], BF16)
        self.KPET = sc("KPET", [64, NT + 256], BF16)
        self.QPET = sc("QPET", [16, 64, NT], BF16)
        self.QNT = sc("QNT", [16, 128, NT], BF16)
        self.KCT = sc("KCT", [8, 128, 256], BF16)
        self.WVB = sc("WVB", [128, 16, 1024], BF16)
        self.WKB = sc("WKB", [128, 16, 1024], BF16)

        sb = lambda n, s, d: es.enter_context(self._sbuf(n, s, d))
        self.ident = sb("ident_sb", [128, 128], F32)
        self.onesf = sb("onesf", [128, 128], F32)
        self.onesb = sb("onesb", [128, 128], BF16)
        self.bar_t = sb("bar_t", [1, 4], F32)
        self.MOD = sb("MOD", [128, DEPTH, 144, 2], F32)
        self.LG = sb("LG", [128, 6, DC], F32)
        self.LB = sb("LB", [128, 6, DC], F32)
        self.GSC = sb("GSC", [128, 2, 6, DC], F32)
        self.G2 = sb("G2", [128, 2, 6, DC], F32)
        self.B2 = sb("B2", [128, 2, 6, DC], F32)
        self.SC1 = sb("SC1", [128, 2, DC], F32)
        self.SH1 = sb("SH1", [128, 2, DC], F32)
        self.PSC = sb("PSC", [128, 8], F32)
        self.QNG = sb("QNG", [128, 4], F32)
        self.KVG = sb("KVG", [128, 4], F32)
        self.PS = [es.enter_context(nc.psum_tensor(f"ps{i}", [128, 512], F32)) for i in range(8)]

        plan = self.plan or ["const", "mod", "ctx", "init", "ffn00", "mix0", "ffn01", "ffn10", "mla", "ffn11", "final"]
        for ph in plan:
            if ph.startswith("ffn"):
                self.ffn_phase(int(ph[3]), int(ph[4]))
            else:
                getattr(self, {"const": "const_phase", "mod": "mod_phase", "conv": "conv_phase", "ctx": "ctx_prep",
                               "init": "init_phase", "mix0": "mix0_phase", "mla": "mla_phase",
                               "final": "final_phase"}.get(ph, ph))()

    def conv_phase(self):
        q = self.POOL
        self.cv = {}

        def conv_kn(dst, src, key, cols):
            ds = self.ds("cv_" + key)
            K = src.shape[0]
            for kc in range(K // 128):
                s = src[kc * 128:(kc + 1) * 128, :].rearrange("p (f c) -> p f c", c=cols)
                d_ = dst[:, :, kc, :].rearrange("f p c -> p f c")
                self.dma(q, d_, s, ds, nogate=True)
            self.cv[key] = (ds, ds.cnt)

        def conv_ffn(l, j):
            conv_kn(self.W1B[l][j], self.ffn_w1[l, j], f"f{l}{j}a", 256)
            conv_kn(self.W3B[l][j], self.ffn_w3[l, j], f"f{l}{j}a", 256)
            q.wait(self.cv[f"f{l}{j}a"])
            conv_kn(self.W2B[l][j], self.ffn_w2[l, j], f"f{l}{j}b", 128)
            q.wait(self.cv[f"f{l}{j}b"])

        conv_ffn(0, 0)
        ds = self.ds("cv_m0")
        for kc in range(16):
            rows = self.na_w_in[0, kc * 128:(kc + 1) * 128, :]
            self.dma(q, self.WINB[0:16, :, kc, :].rearrange("f p c -> p f c"),
                     rows[:, 0:2048].rearrange("p (f c) -> p f c", c=128), ds, nogate=True)
            self.dma(q, self.WINB[16:24, :, kc, :].rearrange("f p c -> p f c"),
                     rows[:, 3072:4096].rearrange("p (f c) -> p f c", c=128), ds, nogate=True)
        self.dma(q, self.WKB, self.na_w_in[0, :, 1024:2048].rearrange("(kc p) n -> p kc n", p=128), ds, nogate=True)
        self.dma(q, self.WVB, self.na_w_in[0, :, 2048:3072].rearrange("(kc p) n -> p kc n", p=128), ds, nogate=True)
        self.cv["m0"] = (ds, ds.cnt)
        q.wait(self.cv["m0"])
        conv_kn(self.WO0B, self.mix0_w_out[0], "m0o", 128)
        q.wait(self.cv["m0o"])
        conv_ffn(0, 1)
        conv_ffn(1, 0)
        conv_kn(self.WO1B, self.mla_w_out[0], "m1", 128)
        q.wait(self.cv["m1"])
        conv_ffn(1, 1)

    def load_fm(self, dst, src2d, nrows, ps, rows_tile):
        SP, PE, DVE = self.SP, self.PE, self.DVE
        t = self.dma(SP, rows_tile[0:nrows, :], src2d, self.ds("lfm"))
        PE.wait(t)
        tp = PE.sig(self.nc.tensor.transpose(out=ps[:, 0:nrows], in_=rows_tile[0:nrows, :],
                                              identity=self.ident[0:nrows, 0:nrows]))
        DVE.wait(tp)
        td = DVE.sig(self.nc.vector.tensor_copy(out=dst, in_=ps[:, 0:nrows]))
        SP.wait(td)
        PE.wait(td)
        return td

    def const_phase(self):
        nc = self.nc
        SP, PE, ACT, DVE = self.SP, self.PE, self.ACT, self.DVE
        t = self.dma(SP, self.ident[:], self.identd, self.ds("c0"))
        DVE.sig(nc.vector.memset(self.onesf[:], 1.0))
        tb = DVE.sig(nc.vector.memset(self.onesb[:], 1.0))
        PE.wait(t, tb)
        DVE.wait(t)
        ACT.wait(t, tb)
        with contextlib.ExitStack() as es:
            rows = es.enter_context(nc.sbuf_tensor("c_rows", [128, 128], F32))
            ps = self.PS[0]
            self.load_fm(self.LG[:].rearrange("p a c -> p (a c)"),
                         self.ln_g.rearrange("l s (c p) -> (l s c) p", p=128), 96, ps, rows)
            self.load_fm(self.LB[:].rearrange("p a c -> p (a c)"),
                         self.ln_b.rearrange("l s (c p) -> (l s c) p", p=128), 96, ps, rows)
            self.load_fm(self.PSC[:], self.pool_scale.rearrange("o (c p) -> (o c) p", p=128), 8, ps, rows)
            self.load_fm(self.QNG[:], self.mla_q_norm.rearrange("o (c p) -> (o c) p", p=128), 4, ps, rows)
            self.load_fm(self.KVG[:], self.mla_kv_norm.rearrange("o (c p) -> (o c) p", p=128), 4, ps, rows)
            self.barrier()

    def mod_phase(self):
        nc = self.nc
        SP, PE, ACT, DVE, POOL = self.SP, self.PE, self.ACT, self.DVE, self.POOL
        with contextlib.ExitStack() as es:
            sb = lambda n, s, d: es.enter_context(self._sbuf(n, s, d))
            rows = sb("m_rows", [128, 128], F32)
            cT = sb("m_cT", [128, 32], F32)
            scT = sb("m_scT", [128, 16, 2], BF16)
            bias = sb("m_bias", [128, DEPTH, 144], F32)
            wst = [sb(f"m_w{i}", [128, 16, 1024], BF16) for i in range(2)]
            ps = self.PS[0]
            tcT = self.load_fm(cT[:], self.cond.rearrange("j (c p) -> (j c) p", p=128), 32, ps, rows)
            ACT.wait(tcT)
            for l in range(DEPTH):
                bm = self.b_mod[l].rearrange("(n p) -> n p", p=128)
                self.load_fm(bias[:, l, 0:128], bm[0:128, :], 128, ps, rows)
                self.load_fm(bias[:, l, 128:144], bm[128:144, :], 16, ps, rows)
            ta = None
            for j in range(2):
                ta = ACT.sig(nc.scalar.activation(out=scT[:, :, j], in_=cT[:, j * 16:(j + 1) * 16], func=AF.Silu))
            PE.wait(ta)
            self.ebuild_body(es)
            free = [None, None]
            tps = []
            for l in range(DEPTH):
                psm = self.PS[1 + l]
                for st in range(18):
                    sl = (l * 18 + st) % 2
                    POOL.wait(free[sl])
                    src = self.w_mod[l, :, st * 1024:(st + 1) * 1024].rearrange("(kc p) n -> p kc n", p=128)
                    tl = self.dma(POOL, wst[sl][:], src, self.ds(f"sw{sl}"))
                    PE.wait(tl)
                    tp = None
                    for nn in range(8):
                        nch = st * 8 + nn
                        for kc in range(16):
                            ins = nc.tensor.matmul(psm[:, nch * 2:nch * 2 + 2], wst[sl][:, kc, nn * 128:(nn + 1) * 128],
                                                   scT[:, kc, :], start=(kc == 0), stop=(kc == 15))
                    tp = PE.sig(ins)
                    free[sl] = tp
                tps.append(tp)
            self.conv_phase()
            for l in range(DEPTH):
                psm = self.PS[1 + l]
                DVE.wait(tps[l])
                for j in range(2):
                    DVE.chain(nc.vector.tensor_tensor(
                        out=self.MOD[:, l, :, j], in0=psm[:, 0:288].rearrange("p (n j) -> p n j", j=2)[:, :, j],
                        in1=bias[:, l, :], op=ALU.add))
            C = DVE.chain
            for grp in range(2):
                def mv(l, v):
                    return self.MOD[:, l, v * 16:(v + 1) * 16, grp]
                C(nc.vector.tensor_scalar(out=self.SC1[:, grp, :], in0=mv(0, 1), scalar1=1.0, scalar2=None, op0=ALU.add))
                C(nc.vector.tensor_copy(out=self.SH1[:, grp, :], in_=mv(0, 0)))
                for l in range(DEPTH):
                    for s in range(3):
                        i = l * 3 + s
                        cgs = (0.5 if s != 1 else 1.0) / ALPHA
                        C(nc.vector.tensor_scalar(out=self.GSC[:, grp, i, :], in0=mv(l, 3 * s + 2), scalar1=cgs,
                                                  scalar2=None, op0=ALU.mult))
                        if s < 2:
                            nl, ns = l, s + 1
                        elif l + 1 < DEPTH:
                            nl, ns = l + 1, 0
                        else:
                            nl = None
                        if nl is None:
                            C(nc.vector.tensor_copy(out=self.G2[:, grp, i, :], in_=self.LG[:, i, :]))
                            C(nc.vector.tensor_copy(out=self.B2[:, grp, i, :], in_=self.LB[:, i, :]))
                        else:
                            C(nc.vector.scalar_tensor_tensor(out=self.G2[:, grp, i, :], in0=mv(nl, 3 * ns + 1), scalar=1.0,
                                                             in1=self.LG[:, i, :], op0=ALU.add, op1=ALU.mult))
                            C(nc.vector.scalar_tensor_tensor(out=self.B2[:, grp, i, :], in0=mv(nl, 3 * ns + 1), scalar=1.0,
                                                             in1=self.LB[:, i, :], op0=ALU.add, op1=ALU.mult))
                            C(nc.vector.tensor_tensor(out=self.B2[:, grp, i, :], in0=self.B2[:, grp, i, :],
                                                      in1=mv(nl, 3 * ns), op=ALU.add))
            tdv = self.DVE.sig(nc.vector.memset(rows[:], 0.0))
            if "DBGMOD" in self.dbg:
                dd = self.dram_out("DBGMOD", [5, 128, 192])
                SP.wait(tdv)
                for i_, t_ in enumerate((self.GSC, self.G2, self.B2)):
                    self.dma(SP, dd[i_], t_[:].rearrange("p a b c -> p (a b c)"), self.ds("dbgm"))
                self.dma(SP, dd[3][:, 0:96], self.LG[:].rearrange("p a c -> p (a c)"), self.ds("dbgm"))
                self.dma(SP, dd[4][:, 0:96], self.LB[:].rearrange("p a c -> p (a c)"), self.ds("dbgm"))
            self.barrier()

    def init_phase(self):
        nc = self.nc
        SP, PE, ACT, DVE = self.SP, self.PE, self.ACT, self.DVE
        with contextlib.ExitStack() as es:
            sb = lambda n, s, d: es.enter_context(self._sbuf(n, s, d))
            xin = [sb(f"i_x{i}", [128, 4, D], F32) for i in range(2)]
            xo = [sb(f"i_xo{i}", [128, TS], F32) for i in range(2)]
            ho = [sb(f"i_ho{i}", [128, TS], BF16) for i in range(2)]
            xin_free = [None, None]
            xo_free = [None, None]
            ho_free = [None, None]
            ps_free = [None, None]
            n = 0
            for tile in range(NTILE):
                grp = 0 if tile == 0 else 1
                sl = tile % 2
                src = self.xp if tile == 0 else self.xs[(tile - 1) * TS:tile * TS, :]
                SP.wait(xin_free[sl])
                tl = self.dma(SP, xin[sl][:], src.rearrange("(s p) d -> p s d", p=128), self.ds(f"ix{sl}"))
                PE.wait(tl)
                for c in range(DC):
                    b = n % 2
                    n += 1
                    ps = self.PS[b]
                    PE.wait(ps_free[b])
                    for s in range(4):
                        ins = nc.tensor.transpose(out=ps[:, s * 128:(s + 1) * 128], in_=xin[sl][:, s, c * 128:(c + 1) * 128],
                                                  identity=self.ident[:])
                    tp = PE.sig(ins)
                    DVE.wait(tp, xo_free[b])
                    t1 = DVE.sig(nc.vector.tensor_copy(out=xo[b][:], in_=ps[:]))
                    ACT.wait(t1, ho_free[b])
                    t2 = ACT.sig(nc.scalar.activation(out=ho[b][:], in_=xo[b][:], func=AF.Identity,
                                                      scale=self.SC1[:, grp, c:c + 1], bias=self.SH1[:, grp, c:c + 1]))
                    ps_free[b] = t1
                    SP.wait(t1, t2)
                    xo_free[b] = self.dma(SP, self.XT[c, :, tile * TS:(tile + 1) * TS], xo[b][:], self.ds(f"ixo{b}"))
                    ho_free[b] = self.dma(SP, self.HT[c, :, tile * TS:(tile + 1) * TS], ho[b][:], self.ds(f"iho{b}"))
                xin_free[sl] = tp
            self.barrier()

    def epi_alloc(self, es):
        nc = self.nc
        sb = lambda n, s, d: es.enter_context(self._sbuf(n, s, d))
        ep = {}
        ep["w2"] = [sb(f"e_w2{i}", [128, FC, 128], BF16) for i in range(2)]
        ep["xres"] = [sb(f"e_xr{i}", [128, TS], F32) for i in range(2)]
        ep["v"] = sb("e_v", [128, DC, TS], F32)
        ep["sq"] = [sb(f"e_sq{i}", [128, TS], F32) for i in range(2)]
        ep["mean"] = sb("e_mean", [128, TS], F32)
        ep["var"] = sb("e_var", [128, TS], F32)
        ep["rstd"] = sb("e_rstd", [128, TS], F32)
        ep["nmr"] = sb("e_nmr", [128, TS], F32)
        ep["xn"] = [sb(f"e_xn{i}", [128, TS], F32) for i in range(2)]
        ep["xo"] = [sb(f"e_xo{i}", [128, TS], F32) for i in range(2)]
        ep["ho"] = [sb(f"e_ho{i}", [128, TS], BF16) for i in range(2)]
        ep["accv"] = sb("e_accv", [128, TS], F32)
        ep["accq"] = sb("e_accq", [128, TS], F32)
        ep["free"] = {}
        ep["pending"] = []
        return ep

    def down_epilogue(self, ep, tile, nk, rhs_fn, wscr, li, rhs_ready, psb=4, wready=None, prefetch=None):
        nc = self.nc
        SP, PE, ACT, DVE = self.SP, self.PE, self.ACT, self.DVE
        grp = 0 if tile == 0 else 1
        fr = ep["free"]
        tk = slice(tile * TS, (tile + 1) * TS)
        psY = [self.PS[psb], self.PS[psb + 1]]
        psS1, psS2 = self.PS[psb + 2], self.PS[psb + 3]
        v = ep["v"]
        tokV = [None] * DC
        tokSq = [None] * DC

        accv, accq = ep["accv"], ep["accq"]

        def accum(dc):
            b = dc % 2
            if dc == 0:
                DVE.wait(fr.get("acc"))
                DVE.chain(nc.vector.tensor_copy(out=accv[:], in_=v[:, 0, :]))
                DVE.wait(tokSq[dc])
                t = DVE.chain(nc.vector.tensor_copy(out=accq[:], in_=ep["sq"][b][:]))
            else:
                DVE.chain(nc.vector.tensor_tensor(out=accv[:], in0=accv[:], in1=v[:, dc, :], op=ALU.add))
                DVE.wait(tokSq[dc])
                t = DVE.chain(nc.vector.tensor_tensor(out=accq[:], in0=accq[:], in1=ep["sq"][b][:], op=ALU.add))
            fr[f"sq{b}"] = t
            return t

        PE.wait(rhs_ready)
        SP.wait(wready)
        tS = None
        pend = ep["pending"]
        for dc in range(DC):
            b = dc % 2
            for _ in range(2 if dc == 0 else 1):
                if pend:
                    pend.pop(0)()
            SP.wait(fr.get(f"w2{b}"))
            tw = self.dma(SP, ep["w2"][b][:, 0:nk, :], wscr[dc], self.ds(f"ew2{b}"))
            SP.wait(fr.get(f"xr{b}"))
            tx = self.dma(SP, ep["xres"][b][:], self.XT[dc, :, tk], self.ds(f"exr{b}"))
            PE.wait(tw, fr.get(f"psY{b}"))
            for k in range(nk):
                ins = nc.tensor.matmul(psY[b][:], ep["w2"][b][:, k, :], rhs_fn(k), start=(k == 0), stop=(k == nk - 1))
            tY = PE.sig(ins)
            fr[f"w2{b}"] = tY
            DVE.wait(tY, tx)
            tokV[dc] = DVE.sig(nc.vector.scalar_tensor_tensor(
                out=v[:, dc, :], in0=psY[b][:], scalar=self.GSC[:, grp, li, dc:dc + 1], in1=ep["xres"][b][:],
                op0=ALU.mult, op1=ALU.add))
            fr[f"psY{b}"] = tokV[dc]
            fr[f"xr{b}"] = tokV[dc]
            ACT.wait(tokV[dc], fr.get(f"sq{b}"))
            tokSq[dc] = ACT.sig(nc.scalar.activation(out=ep["sq"][b][:], in_=v[:, dc, :], func=AF.Square))
            if dc >= 1:
                accum(dc - 1)
        tacc = accum(DC - 1)
        PE.wait(tacc, fr.get("psS"))
        nc.tensor.matmul(psS1[:], self.onesf[:], accv[:], start=True, stop=True)
        tS = PE.sig(nc.tensor.matmul(psS2[:], self.onesf[:], accq[:], start=True, stop=True))
        fr["acc"] = tS
        if prefetch is not None:
            prefetch()
        def fin_stats():
            DVE.wait(tS)
            invd = 1.0 / D
            C = DVE.chain
            C(nc.vector.tensor_scalar(out=ep["mean"][:], in0=psS1[:], scalar1=invd, scalar2=None, op0=ALU.mult))
            C(nc.vector.tensor_tensor(out=ep["nmr"][:], in0=ep["mean"][:], in1=ep["mean"][:], op=ALU.mult))
            C(nc.vector.scalar_tensor_tensor(out=ep["var"][:], in0=psS2[:], scalar=invd, in1=ep["nmr"][:],
                                             op0=ALU.mult, op1=ALU.subtract))
            tvar = C(nc.vector.tensor_scalar(out=ep["var"][:], in0=ep["var"][:], scalar1=EPS_P, scalar2=None, op0=ALU.add))
            ACT.wait(tvar)
            tsd = ACT.sig(nc.scalar.activation(out=ep["rstd"][:], in_=ep["var"][:], func=AF.Sqrt))
            DVE.wait(tsd)
            C(nc.vector.reciprocal(out=ep["rstd"][:], in_=ep["rstd"][:]))
            tR = C(nc.vector.scalar_tensor_tensor(out=ep["nmr"][:], in0=ep["mean"][:], scalar=-1.0, in1=ep["rstd"][:],
                                                  op0=ALU.mult, op1=ALU.mult))
            fr["psS"] = tR

        def fin_dc(dc):
            b = dc % 2
            DVE.wait(fr.get(f"xn{b}"))
            DVE.chain(nc.vector.tensor_tensor(out=ep["xn"][b][:], in0=v[:, dc, :], in1=ep["rstd"][:], op=ALU.mult))
            tn = DVE.chain(nc.vector.tensor_tensor(out=ep["xn"][b][:], in0=ep["xn"][b][:], in1=ep["nmr"][:], op=ALU.add))
            ACT.wait(tn, fr.get(f"xo{b}"))
            ta = ACT.sig(nc.scalar.activation(out=ep["xo"][b][:], in_=ep["xn"][b][:], func=AF.Identity,
                                              scale=self.LG[:, li, dc:dc + 1], bias=self.LB[:, li, dc:dc + 1]))
            DVE.wait(fr.get(f"ho{b}"))
            th = DVE.sig(nc.vector.tensor_scalar(out=ep["ho"][b][:], in0=ep["xn"][b][:],
                                                 scalar1=self.G2[:, grp, li, dc:dc + 1],
                                                 scalar2=self.B2[:, grp, li, dc:dc + 1], op0=ALU.mult, op1=ALU.add))
            fr[f"xn{b}"] = [ta, th]
            SP.wait(ta)
            fr[f"xo{b}"] = self.dma(SP, self.XT[dc, :, tk], ep["xo"][b][:], self.ds(f"exo{b}"))
            SP.wait(th)
            fr[f"ho{b}"] = self.dma(SP, self.HT[dc, :, tk], ep["ho"][b][:], self.ds(f"eho{b}"))

        pend.append(fin_stats)
        for dc in range(DC):
            pend.append(lambda dc=dc: fin_dc(dc))

    def epi_flush(self, ep):
        while ep["pending"]:
            ep["pending"].pop(0)()

    def ffn_phase(self, l, j):
        nc = self.nc
        SP, PE, ACT, DVE = self.SP, self.PE, self.ACT, self.DVE
        li = l * 3 + (0 if j == 0 else 2)
        W1, W3, W2 = self.W1B[l][j], self.W3B[l][j], self.W2B[l][j]
        cvt = self.cv[f"f{l}{j}a"]
        with contextlib.ExitStack() as es:
            sb = lambda n, s, d: es.enter_context(self._sbuf(n, s, d))
            hT = sb("f_hT", [128, DC, TS], BF16)
            g = sb("f_g", [128, FC, TS], BF16)
            w1s = [sb(f"f_w1{i}", [128, 16, 256], BF16) for i in range(2)]
            w3s = [sb(f"f_w3{i}", [128, 16, 256], BF16) for i in range(2)]
            sg = [sb(f"f_sg{i}", [128, TS], F32) for i in range(2)]
            ep = self.epi_alloc(es)
            SP.wait(cvt)
            wfree = [None, None]
            ps1f = [None, None]
            ps3f = [None, None]
            sgf = [None, None]
            stt = {"hfree": None}
            pre = {}

            def issue_h(tile):
                SP.wait(stt["hfree"])
                return self.dma(SP, hT[:], self.HT[:, :, tile * TS:(tile + 1) * TS].rearrange("c p t -> p c t"),
                                self.ds("fh"))

            def issue_w(fp):
                sl = fp % 2
                SP.wait(wfree[sl])
                self.dma(SP, w1s[sl][:], W1[fp], self.ds(f"fw{sl}"))
                return self.dma(SP, w3s[sl][:], W3[fp], self.ds(f"fw{sl}"))

            for tile in range(NTILE):
                th = pre.pop("h", None) or issue_h(tile)
                PE.wait(th)
                tG = None
                for fp in range(22):
                    sl = fp % 2
                    tw = pre.pop(fp, None) or issue_w(fp)
                    for _ in range(2 if fp == 0 else 1):
                        if ep["pending"]:
                            ep["pending"].pop(0)()
                    PE.wait(tw)
                    for f2 in range(2):
                        f = fp * 2 + f2
                        b = f % 2
                        ps1, ps3 = self.PS[b], self.PS[2 + b]
                        PE.wait(ps1f[b])
                        for kc in range(16):
                            ins = nc.tensor.matmul(ps1[:], w1s[sl][:, kc, f2 * 128:(f2 + 1) * 128], hT[:, kc, :],
                                                   start=(kc == 0), stop=(kc == 15))
                        t1 = PE.sig(ins)
                        PE.wait(ps3f[b])
                        for kc in range(16):
                            ins = nc.tensor.matmul(ps3[:], w3s[sl][:, kc, f2 * 128:(f2 + 1) * 128], hT[:, kc, :],
                                                   start=(kc == 0), stop=(kc == 15))
                        t3 = PE.sig(ins)
                        ACT.wait(t1, sgf[b])
                        ts_ = ACT.sig(nc.scalar.activation(out=sg[b][:], in_=ps1[:], func=AF.Silu))
                        ps1f[b] = ts_
                        DVE.wait(ts_, t3)
                        tG = DVE.sig(nc.vector.tensor_tensor(out=g[:, f, :], in0=sg[b][:], in1=ps3[:], op=ALU.mult))
                        ps3f[b] = tG
                        sgf[b] = tG
                    wfree[sl] = t3
                stt["hfree"] = t3

                def prefetch(tile=tile):
                    if tile + 1 < NTILE:
                        pre["h"] = issue_h(tile + 1)
                        pre[0] = issue_w(0)
                        pre[1] = issue_w(1)
                self.down_epilogue(ep, tile, FC, lambda k: g[:, k, :], W2, li, tG, psb=4,
                                   wready=self.cv[f"f{l}{j}b"], prefetch=prefetch)
            self.epi_flush(ep)
            self.barrier()

    def outproj_phase(self, src, wscr, li, cvt):
        nc = self.nc
        SP, PE = self.SP, self.PE
        with contextlib.ExitStack() as es:
            sb = lambda n, s, d: es.enter_context(self._sbuf(n, s, d))
            cT = [sb(f"o_c{i}", [128, DC, TS], BF16) for i in range(2)]
            ep = self.epi_alloc(es)
            SP.wait(cvt)
            cfree = [None, None]
            pre = {}

            def issue_c(tile):
                sl = tile % 2
                SP.wait(cfree[sl])
                return self.dma(SP, cT[sl][:], src[:, :, tile * TS:(tile + 1) * TS].rearrange("c p t -> p c t"),
                                self.ds(f"oc{sl}"))

            for tile in range(NTILE):
                sl = tile % 2
                tl = pre.pop("c", None) or issue_c(tile)

                def prefetch(tile=tile):
                    if tile + 1 < NTILE:
                        pre["c"] = issue_c(tile + 1)
                self.down_epilogue(ep, tile, 16, lambda k, sl=sl: cT[sl][:, k, :], wscr, li, tl, psb=4, prefetch=prefetch)
                cfree[sl] = (self.PE, self.PE.cnt)
            self.epi_flush(ep)
            self.barrier()

    def mix0_phase(self):
        self.mix0_inproj()
        self.mix0_pool()
        self.mix0_attn_prompt()
        self.mix0_attn_sample()
        self.outproj_phase(self.CT, self.WO0B, 1, self.cv["m0o"])

    def mla_phase(self):
        self.mla_down()
        self.mla_qproj()
        self.mla_attn()
        self.outproj_phase(self.CT, self.WO1B, 4, self.cv["m1"])

    def final_phase(self):
        nc = self.nc
        SP, PE, ACT, DVE = self.SP, self.PE, self.ACT, self.DVE
        with contextlib.ExitStack() as es:
            sb = lambda n, s, d: es.enter_context(self._sbuf(n, s, d))
            xt = [sb(f"z_x{i}", [128, DC, TS], F32) for i in range(2)]
            yo = [sb(f"z_y{i}", [128, D], F32) for i in range(2)]
            xfree = [None, None]
            yfree = [None, None]
            psf = [None] * 4
            n = 0
            m = 0
            for tile in range(NTILE):
                sl = tile % 2
                tk = slice(tile * TS, (tile + 1) * TS)
                SP.wait(xfree[sl])
                tl = self.dma(SP, xt[sl][:], self.XT[:, :, tk].rearrange("c p t -> p c t"), self.ds(f"zx{sl}"))
                PE.wait(tl)
                for s in range(4):
                    yb = m % 2
                    m += 1
                    tlast = []
                    for q4 in range(4):
                        b = n % 4
                        n += 1
                        ps = self.PS[b]
                        PE.wait(psf[b])
                        for cc in range(4):
                            c = q4 * 4 + cc
                            ins = nc.tensor.transpose(out=ps[:, cc * 128:(cc + 1) * 128],
                                                      in_=xt[sl][:, c, s * 128:(s + 1) * 128], identity=self.ident[:])
                        tp = PE.sig(ins)
                        eng = DVE if q4 % 2 == 0 else ACT
                        eng.wait(tp, yfree[yb])
                        if eng is DVE:
                            tcp = DVE.sig(nc.vector.tensor_copy(out=yo[yb][:, q4 * 512:(q4 + 1) * 512], in_=ps[:]))
                        else:
                            tcp = ACT.sig(nc.scalar.copy(out=yo[yb][:, q4 * 512:(q4 + 1) * 512], in_=ps[:]))
                        psf[b] = tcp
                        tlast.append(tcp)
                    SP.wait(tlast)
                    if tile == 0:
                        dst = self.yp[s * 128:(s + 1) * 128, :]
                    else:
                        r0 = (tile - 1) * TS + s * 128
                        dst = self.ys[r0:r0 + 128, :]
                    yfree[yb] = self.dma(SP, dst, yo[yb][:], self.ds(f"zy{yb}"))
                xfree[sl] = tp
            self.barrier()


def _consts():
    c = {}
    c["ident"] = np.eye(128, dtype=np.float32)
    m = np.zeros((3, 8, 128, 512), np.float32)
    cc = np.arange(64)
    cs = np.clip(cc - 8, 0, 48)
    jj = np.arange(64)
    colv = ((jj[:, None] >= cs[None, :]) & (jj[:, None] < cs[None, :] + 16)).astype(np.float32)
    base = [(0, 0), (20, 24), (48, 56)]
    for var in range(3):
        i0, r0 = base[var]
        for k in range(8):
            for t in range(2):
                ia = i0 + 2 * k + t
                for mm in range(8):
                    r = r0 + mm
                    rs = min(max(r - 4, 0), 56)
                    if rs <= ia < rs + 8:
                        m[var, k, t * 64:(t + 1) * 64, mm * 64:(mm + 1) * 64] = colv
    c["emask"] = m
    inv = np.ones((4, UTLEN), np.float32)
    segs = [(0, 256), (272, 256), (544, NSA)]
    for g, w in enumerate((2, 4, 8, 16)):
        for off, L in segs:
            t = np.arange(L)
            cnt = np.minimum(t + w // 2, L) - np.maximum(t - w // 2, 0)
            inv[g, off + UPAD:off + UPAD + L] = (1.0 / cnt.astype(np.float32)).astype(np.float32)
    c["invcnt"] = np.ascontiguousarray(np.broadcast_to(inv[:, None, :], (4, 128, UTLEN)))
    t = np.arange(NSA)
    invf = (10000.0 ** (-np.arange(0, 32, 2, dtype=np.float32) / 32)).astype(np.float32)
    ang_row = (t // GRID_W).astype(np.float32)[None, :] * invf[:, None]
    ang_col = (t % GRID_W).astype(np.float32)[None, :] * invf[:, None]
    cosr, sinr = np.cos(ang_row), np.sin(ang_row)
    cosc, sinc = np.cos(ang_col), np.sin(ang_col)
    c["ropec"] = np.concatenate([cosr, cosr, cosc, cosc], 0).astype(np.float32)
    c["ropes"] = np.concatenate([-sinr, sinr, -sinc, sinc], 0).astype(np.float32)
    return c


_CONSTS = None


def make_in_maps(inp, cores=range(8)):
    global _CONSTS
    if _CONSTS is None:
        _CONSTS = _consts()
    f = lambda a: np.ascontiguousarray(np.asarray(a, dtype=np.float32))
    shared = {k: f(inp[k]) for k in ("w_mod", "b_mod", "ln_g", "ln_b", "ffn_w1", "ffn_w3", "ffn_w2", "na_w_in",
                                     "mix0_w_out", "pool_w", "pool_scale", "mla_w_down", "mla_q_norm", "mla_w_uq",
                                     "mla_kv_norm", "mla_w_ukv", "mla_w_out")}
    shared.update(_CONSTS)
    rpb = f(inp["na_rpb"])[0]
    rr = np.zeros((8, 31, 127), np.float32)
    rr[:, 8:23, 48:79] = rpb[:, ::-1, ::-1]
    shared["rpbrep"] = np.ascontiguousarray(np.broadcast_to(rr[:, :, None, :], (8, 31, 64, 127))).reshape(-1)
    maps = []
    for i in cores:
        m = dict(shared)
        m["xp"] = f(inp["x_prompt"][2 * i:2 * i + 2]).reshape(NPR, D)
        m["xs"] = f(inp["x_sample"][i])
        m["cnak"] = f(inp["cache_na_k"][i, 0]).reshape(256, 1024)
        m["cnav"] = f(inp["cache_na_v"][i, 0]).reshape(256, 1024)
        m["cckv"] = f(inp["cache_mla_ckv"][i, 0])
        m["ckpe"] = f(inp["cache_mla_kpe"][i, 0])
        m["cond"] = np.ascontiguousarray(np.stack([f(inp["c_ctx"]), f(inp["c"][i])], 0))
        maps.append(m)
    return maps


_NC = None
_KBO = None


def kernel(**inputs):
    global _NC, _KBO
    if _NC is None:
        _KBO = KB()
        _NC = _KBO.build()
    maps = make_in_maps(inputs)
    maps = [{k: m[k] for k in _KBO.ins} for m in maps]
    res = run_bass_kernel_spmd(_NC, maps, core_ids=list(range(8)))
    B, S = 16, 256
    y_prompt = np.zeros((B, S, D), np.float32)
    y_sample = np.zeros((8, NSA, D), np.float32)
    nk = np.zeros((B, 1, S, 8, 128), np.float32)
    nv = np.zeros((B, 1, S, 8, 128), np.float32)
    nckv = np.zeros((B, 1, S, 512), np.float32)
    nkpe = np.zeros((B, 1, S, 64), np.float32)
    for i, r in enumerate(res.results):
        y_prompt[2 * i:2 * i + 2] = np.asarray(r["yp"]).reshape(2, S, D)
        y_sample[i] = np.asarray(r["ys"])
        nk[2 * i:2 * i + 2, 0] = np.asarray(r["onk"]).reshape(2, S, 8, 128)
        nv[2 * i:2 * i + 2, 0] = np.asarray(r["onv"]).reshape(2, S, 8, 128)
        nckv[2 * i:2 * i + 2, 0] = np.asarray(r["onckv"]).reshape(2, S, 512)
        nkpe[2 * i:2 * i + 2, 0] = np.asarray(r["onkpe"]).reshape(2, S, 64)
    return (y_prompt, y_sample, nk, nv, nckv, nkpe)


def _units():
    u = [(UPAD, 256, 0), (272 + UPAD, 256, 256)]
    for i in range(NSA // TS):
        u.append((544 + UPAD + i * TS, TS, NPR + i * TS))
    return u


def ctx_prep(self):
    nc = self.nc
    SP, PE, ACT, DVE = self.SP, self.PE, self.ACT, self.DVE
    with contextlib.ExitStack() as es:
        sb = lambda n, s, d: es.enter_context(self._sbuf(n, s, d))
        src = sb("cp_src", [128, 2, 1024], F32)
        ob = sb("cp_ob", [128, 16, 256], BF16)
        jobs = [(self.cnak, 1024, "nak"), (self.cckv, 512, "ckv"), (self.ckpe, 64, "kpe")]
        for (dr, width, nm) in jobs:
            t = self.dma(SP, src[:, :, 0:width], dr.rearrange("(kc p) d -> p kc d", p=128), self.ds("cp"))
            PE.wait(t)
            nch = (width + 127) // 128
            for c in range(nch):
                w = min(128, width - c * 128)
                ps = self.PS[c % 2]
                for kc in range(2):
                    ins = nc.tensor.transpose(out=ps[0:w, kc * 128:(kc + 1) * 128], in_=src[:, kc, c * 128:c * 128 + w],
                                              identity=self.ident[:])
                tp = PE.sig(ins)
                DVE.wait(tp)
                td = DVE.sig(nc.vector.tensor_copy(out=ob[0:w, c, :], in_=ps[0:w, 0:256]))
                PE.wait(td)
            SP.wait(td)
            if nm == "nak":
                self.dma(SP, self.KCT.rearrange("h p t -> p h t"), ob[:, 0:8, :], self.ds("cp2"))
            elif nm == "ckv":
                self.dma(SP, self.CKVT[:, :, NT:NT + 256].rearrange("c p t -> p c t"), ob[:, 0:4, :], self.ds("cp2"))
            else:
                self.dma(SP, self.KPET[:, NT:NT + 256], ob[0:64, 0, :], self.ds("cp2"))
            SP.wait((self.ds("cp2"), self.ds("cp2").cnt))
        self.barrier()


class AttnState:
    pass


def attn_setup(self, es, nqmax):
    nc = self.nc
    st = AttnState()
    sb = lambda n, s, d: es.enter_context(self._sbuf(n, s, d))
    st.pT = [sb(f"a_pT{i}", [128, nqmax], BF16) for i in range(3)]
    st.rec = [sb(f"a_rec{i}", [128, nqmax], F32) for i in range(2)]
    st.acc = [sb(f"a_acc{i}", [128, nqmax], F32) for i in range(2)]
    st.accfree = [None] * 2
    st.o = [sb(f"a_o{i}", [128, nqmax], BF16) for i in range(2)]
    st.sfree = [None] * 3
    st.pfree = [None] * 3
    st.ofree = [None] * 2
    st.recfree = [None] * 2
    st.ostore = [None] * 2
    st.nS = 0
    st.nO = 0
    return st


def attn_block(self, st, nq, chunks, scale, dst, ready, dacc=False):
    nc = self.nc
    SP, PE, ACT, DVE = self.SP, self.PE, self.ACT, self.DVE
    ob = st.nO % 2
    st.nO += 1
    psO, psD = self.PS[4 + 2 * ob], self.PS[5 + 2 * ob]
    n = len(chunks)
    PE.wait(ready)
    slots = []

    def qk(ci):
        s = st.nS % 3
        st.nS += 1
        slots.append(s)
        PE.wait(st.sfree[s])
        kq = chunks[ci]["kq"]
        for i, (l_, r_) in enumerate(kq):
            ins = nc.tensor.matmul(self.PS[s][:, 0:nq], l_, r_, start=(i == 0), stop=(i == len(kq) - 1))
        return PE.sig(ins)

    tq = [None] * n
    tq[0] = qk(0)
    tpv = None
    for ci in range(n):
        if ci + 1 < n:
            tq[ci + 1] = qk(ci + 1)
        s = slots[ci]
        ACT.wait(tq[ci], st.pfree[s])
        te = ACT.sig(nc.scalar.activation(out=st.pT[s][:, 0:nq], in_=self.PS[s][:, 0:nq], func=AF.Exp, scale=scale))
        st.sfree[s] = te
        e = chunks[ci].get("e")
        if e is not None:
            DVE.wait(te)
            te = DVE.sig(nc.vector.tensor_tensor(out=st.pT[s][:, 0:nq], in0=st.pT[s][:, 0:nq], in1=e, op=ALU.mult))
        PE.wait(te)
        if ci == 0:
            PE.wait(st.ofree[ob])
        if not dacc:
            nc.tensor.matmul(psO[:, 0:nq], chunks[ci]["v"], st.pT[s][:, 0:nq], start=(ci == 0), stop=(ci == n - 1))
            tpv = PE.sig(nc.tensor.matmul(psD[:, 0:nq], self.onesb[:], st.pT[s][:, 0:nq], start=(ci == 0),
                                          stop=(ci == n - 1)))
            st.pfree[s] = tpv
        else:
            tpv = PE.sig(nc.tensor.matmul(psO[:, 0:nq], chunks[ci]["v"], st.pT[s][:, 0:nq], start=(ci == 0),
                                          stop=(ci == n - 1)))
            acc = st.acc[ob]
            DVE.wait(te)
            if ci == 0:
                DVE.wait(st.accfree[ob])
                tacc = DVE.chain(nc.vector.tensor_copy(out=acc[:, 0:nq], in_=st.pT[s][:, 0:nq]))
            else:
                tacc = DVE.chain(nc.vector.tensor_tensor(out=acc[:, 0:nq], in0=acc[:, 0:nq], in1=st.pT[s][:, 0:nq],
                                                         op=ALU.add))
            st.pfree[s] = [tpv, tacc]
    if dacc:
        PE.wait(tacc)
        tpv = PE.sig(nc.tensor.matmul(psD[:, 0:nq], self.onesf[:], st.acc[ob][:, 0:nq], start=True, stop=True))
        st.accfree[ob] = tpv
    DVE.wait(tpv, st.recfree[ob])
    DVE.chain(nc.vector.reciprocal(out=st.rec[ob][:, 0:nq], in_=psD[:, 0:nq]))
    DVE.wait(st.ostore[ob])
    to = DVE.sig(nc.vector.tensor_tensor(out=st.o[ob][:, 0:nq], in0=psO[:, 0:nq], in1=st.rec[ob][:, 0:nq], op=ALU.mult))
    st.ofree[ob] = to
    st.recfree[ob] = to
    SP.wait(to)
    st.ostore[ob] = self.dma(SP, dst, st.o[ob][:, 0:nq], self.ds(f"ao{ob}"))
    return tpv


def mix0_inproj(self):
    nc = self.nc
    SP, PE, ACT, DVE, POOL = self.SP, self.PE, self.ACT, self.DVE, self.POOL
    with contextlib.ExitStack() as es:
        sb = lambda n, s, d: es.enter_context(self._sbuf(n, s, d))
        hT = [sb(f"p_h{i}", [128, DC, TS], BF16) for i in range(2)]
        wst = [sb(f"p_w{i}", [128, 16, 128], BF16) for i in range(3)]
        wv = sb("p_wv", [128, 16, 1024], BF16)
        wk = sb("p_wk", [128, 16, 1024], BF16)
        ob = [sb(f"p_ob{i}", [128, TS], BF16) for i in range(2)]
        of = [sb(f"p_of{i}", [128, TS], F32) for i in range(2)]
        zero = sb("p_zero", [128, 16], F32)
        SP.wait(self.cv["m0"])
        tz = DVE.sig(nc.vector.memset(zero[:], 0.0))
        SP.wait(tz)
        for (p0, n_, t0) in _units():
            if n_ == 256 or t0 == NPR:
                self.dma(SP, self.UT[:, :, p0 - UPAD:p0].rearrange("c p t -> p c t"),
                         zero[:].rearrange("p (c t) -> p c t", t=UPAD)[:, 0:8, :] if False else
                         zero[:, 0:UPAD].unsqueeze(1).to_broadcast([128, 8, UPAD]), self.ds("pz"))
            if n_ == 256 or t0 == NT - TS:
                self.dma(SP, self.UT[:, :, p0 + n_:p0 + n_ + UPAD].rearrange("c p t -> p c t"),
                         zero[:, 0:UPAD].unsqueeze(1).to_broadcast([128, 8, UPAD]), self.ds("pz"))
        self.dma(SP, wv[:], self.WVB, self.ds("pwv"))
        twk = self.dma(SP, wk[:], self.WKB, self.ds("pwv"))
        twv = twk
        hfree = [None, None]
        wfree = [None, None, None]
        psf = [None] * 4
        obf = [None, None]
        off = [None, None]
        n = 0
        nv = 0
        units = _units()
        wtok = {}
        htok = {}

        def issue_w(idx):
            ws = idx % 3
            SP.wait(wfree[ws])
            wtok[idx] = self.dma(SP, wst[ws][:], self.WINB[idx % 24], self.ds(f"pw{ws}"))

        def issue_h(tile):
            sl_ = tile % 2
            SP.wait(hfree[sl_])
            htok[tile] = self.dma(SP, hT[sl_][:], self.HT[:, :, tile * TS:(tile + 1) * TS].rearrange("c p t -> p c t"),
                                  self.ds(f"ph{sl_}"))

        issue_h(0)
        issue_w(0)
        issue_w(1)
        for tile in range(NTILE):
            sl = tile % 2
            tk = slice(tile * TS, (tile + 1) * TS)
            if tile + 1 < NTILE:
                issue_h(tile + 1)
            th = htok.pop(tile)
            PE.wait(th)
            for cc in range(24):
                idx = tile * 24 + cc
                ws = idx % 3
                if idx + 2 < NTILE * 24:
                    issue_w(idx + 2)
                tw = wtok.pop(idx)
                b = n % 2
                n += 1
                ps = self.PS[b]
                PE.wait(tw, psf[b])
                for kc in range(16):
                    ins = nc.tensor.matmul(ps[:], wst[ws][:, kc, :], hT[sl][:, kc, :], start=(kc == 0), stop=(kc == 15))
                tp = PE.sig(ins)
                wfree[ws] = tp
                if cc < 16:
                    ACT.wait(tp, obf[b])
                    te = ACT.sig(nc.scalar.copy(out=ob[b][:], in_=ps[:]))
                    psf[b] = te
                    SP.wait(te)
                    dst = (self.QT if cc < 8 else self.KT)[cc % 8, :, tk]
                    obf[b] = self.dma(SP, dst, ob[b][:], self.ds(f"pob{b}"))
                else:
                    DVE.wait(tp, off[b])
                    te = DVE.sig(nc.vector.tensor_copy(out=of[b][:], in_=ps[:]))
                    psf[b] = te
                    SP.wait(te)
                    c = cc - 16
                    if tile == 0:
                        self.dma(SP, self.UT[c, :, units[0][0]:units[0][0] + 256], of[b][:, 0:256], self.ds(f"pof{b}"))
                        off[b] = self.dma(SP, self.UT[c, :, units[1][0]:units[1][0] + 256], of[b][:, 256:512],
                                          self.ds(f"pof{b}"))
                    else:
                        p0 = units[tile + 1][0]
                        off[b] = self.dma(SP, self.UT[c, :, p0:p0 + TS], of[b][:], self.ds(f"pof{b}"))
            PE.wait(twv, twk)
            for kind in (("v", "k") if tile == 0 else ("v",)):
                wsrc = wv if kind == "v" else wk
                for s in range(4):
                    for half in range(2):
                        b = 2 + nv % 2
                        nv += 1
                        ps = self.PS[b]
                        PE.wait(psf[b])
                        for kc in range(16):
                            ins = nc.tensor.matmul(ps[:], hT[sl][:, kc, s * 128:(s + 1) * 128],
                                                   wsrc[:, kc, half * 512:(half + 1) * 512],
                                                   start=(kc == 0), stop=(kc == 15))
                        tp = PE.sig(ins)
                        r0 = tile * TS + s * 128
                        bb = b - 2
                        if tile == 0:
                            DVE.wait(tp, off[bb])
                            tdv = DVE.sig(nc.vector.tensor_copy(out=of[bb][:], in_=ps[:]))
                            psf[b] = tdv
                            tuse = [tdv]
                            if kind == "v":
                                ACT.wait(tdv, obf[bb])
                                te = ACT.sig(nc.scalar.copy(out=ob[bb][:], in_=of[bb][:]))
                                SP.wait(te)
                                obf[bb] = self.dma(SP, self.VS[r0:r0 + 128, half * 512:(half + 1) * 512], ob[bb][:],
                                                   self.ds(f"pob{bb}"))
                                tuse.append(te)
                            SP.wait(tdv)
                            dsto = self.onv if kind == "v" else self.onk
                            td_ = self.dma(SP, dsto[s * 128:(s + 1) * 128, half * 512:(half + 1) * 512], of[bb][:],
                                           self.ds(f"pof{bb}"))
                            off[bb] = [td_] + tuse
                        else:
                            ACT.wait(tp, obf[bb])
                            te = ACT.sig(nc.scalar.copy(out=ob[bb][:], in_=ps[:]))
                            psf[b] = te
                            SP.wait(te)
                            obf[bb] = self.dma(SP, self.VS[r0:r0 + 128, half * 512:(half + 1) * 512], ob[bb][:],
                                               self.ds(f"pob{bb}"))
            hfree[sl] = tp
        self.barrier()


def mix0_pool(self):
    nc = self.nc
    SP, PE, ACT, DVE, POOL = self.SP, self.PE, self.ACT, self.DVE, self.POOL
    with contextlib.ExitStack() as es:
        sb = lambda n, s, d: es.enter_context(self._sbuf(n, s, d))
        pw = sb("q_pw", [128, 8, 256], BF16)
        u = [sb(f"q_u{i}", [128, 8, TS + 16], F32) for i in range(2)]
        inv = [sb(f"q_inv{i}", [128, 4, TS], F32) for i in range(2)]
        t1 = sb("q_t1", [128, TS + 16], F32)
        t2 = sb("q_t2", [128, TS + 16], F32)
        dT = [sb(f"q_d{i}", [128, 8, TS], BF16) for i in range(2)]
        ob = [sb(f"q_ob{i}", [128, TS], BF16) for i in range(2)]
        tpw = self.dma(POOL, pw[:], self.pool_w[0].rearrange("g (cc p) d -> p (g cc) d", p=128), self.ds("sw0"))
        ufree = [None, None]
        dfree = [None, None]
        psf = [None, None]
        obf = [None, None]
        n = 0
        for ui, (p0, nn, t0) in enumerate(_units()):
            sl = ui % 2
            SP.wait(ufree[sl])
            self.dma(SP, u[sl][:, :, 0:nn + 16], self.UT[:, :, p0 - UPAD:p0 + nn + UPAD].rearrange("c p t -> p c t"),
                     self.ds(f"qu{sl}"))
            tl = self.dma(SP, inv[sl][:, :, 0:nn], self.invcnt[:, :, p0:p0 + nn].rearrange("g p t -> p g t"),
                          self.ds(f"qu{sl}"))
            DVE.wait(tl, dfree[sl])
            td = None
            for c in range(8):
                g = c // 2
                w = (2, 4, 8, 16)[g]
                uu = u[sl][:, c, :]
                L = nn + 15
                DVE.chain(nc.vector.tensor_tensor(out=t1[:, 0:L], in0=uu[:, 0:L], in1=uu[:, 1:L + 1], op=ALU.add))
                cur, oth = t1, t2
                step = 2
                while step < w:
                    L2 = L - step
                    DVE.chain(nc.vector.tensor_tensor(out=oth[:, 0:L2], in0=cur[:, 0:L2], in1=cur[:, step:step + L2],
                                                      op=ALU.add))
                    cur, oth = oth, cur
                    L = L2
                    step *= 2
                o0 = UPAD - w // 2
                DVE.chain(nc.vector.tensor_tensor(out=oth[:, 0:nn], in0=cur[:, o0:o0 + nn], in1=inv[sl][:, g, 0:nn],
                                                  op=ALU.mult))
                td = DVE.chain(nc.vector.tensor_tensor(out=dT[sl][:, c, 0:nn], in0=oth[:, 0:nn], in1=uu[:, UPAD:UPAD + nn],
                                                       op=ALU.subtract))
            ufree[sl] = td
            PE.wait(td, tpw)
            tp = None
            for g in range(4):
                for dch in range(2):
                    b = n % 2
                    n += 1
                    ps = self.PS[b]
                    PE.wait(psf[b])
                    for cc in range(2):
                        ins = nc.tensor.matmul(ps[:, 0:nn], pw[:, g * 2 + cc, dch * 128:(dch + 1) * 128],
                                               dT[sl][:, g * 2 + cc, 0:nn], start=(cc == 0), stop=(cc == 1))
                    tp = PE.sig(ins)
                    ACT.wait(tp, obf[b])
                    oc = g * 2 + dch
                    te = ACT.sig(nc.scalar.activation(out=ob[b][:, 0:nn], in_=ps[:, 0:nn], func=AF.Identity,
                                                      scale=self.PSC[:, oc:oc + 1]))
                    psf[b] = te
                    SP.wait(te)
                    obf[b] = self.dma(SP, self.CT[8 + oc, :, t0:t0 + nn], ob[b][:, 0:nn], self.ds(f"qob{b}"))
            dfree[sl] = tp
        self.barrier()


def mix0_attn_prompt(self):
    nc = self.nc
    SP, PE = self.SP, self.PE
    with contextlib.ExitStack() as es:
        sb = lambda n, s, d: es.enter_context(self._sbuf(n, s, d))
        st = attn_setup(self, es, TS)
        qa = [sb(f"r_q{i}", [128, 8, 256], BF16) for i in range(2)]
        ka = [sb(f"r_k{i}", [128, 8, 256], BF16) for i in range(2)]
        va = [sb(f"r_v{i}", [128, 2, 1024], BF16) for i in range(2)]
        for s in range(2):
            tk = slice(s * 256, (s + 1) * 256)
            self.dma(SP, qa[s][:], self.QT[:, :, tk].rearrange("h p t -> p h t"), self.ds(f"rl{s}"))
            self.dma(SP, ka[s][:], self.KT[:, :, tk].rearrange("h p t -> p h t"), self.ds(f"rl{s}"))
            tl = self.dma(SP, va[s][:], self.VS[s * 256:(s + 1) * 256, :].rearrange("(kc p) d -> p kc d", p=128),
                          self.ds(f"rl{s}"))
            for h in range(8):
                chunks = [dict(kq=[(ka[s][:, h, kc * 128:(kc + 1) * 128], qa[s][:, h, :])],
                               v=va[s][:, kc, h * 128:(h + 1) * 128]) for kc in range(2)]
                attn_block(self, st, 256, chunks, 128 ** -0.5, self.CT[h, :, tk], tl)
        self.barrier()


E_VALID = {0: list(range(0, 6)), 1: list(range(0, 8)), 2: list(range(2, 8))}
E_ABASE = {0: 15, 1: 19, 2: 23}


def mix0_ebuild(self):
    with contextlib.ExitStack() as es:
        self.ebuild_body(es)
        self.barrier()


def ebuild_body(self, es):
    nc = self.nc
    SP, PE, ACT, DVE = self.SP, self.PE, self.ACT, self.DVE
    if True:
        sb = lambda n, s, d: es.enter_context(self._sbuf(n, s, d))
        raw = [sb(f"b_raw{i}", [128, TS], F32) for i in range(2)]
        ex = [sb(f"b_ex{i}", [128, TS], F32) for i in range(2)]
        mk = [sb(f"b_mk{i}", [128, TS], F32) for i in range(2)]
        eo = [sb(f"b_eo{i}", [128, TS], BF16) for i in range(2)]
        rawf = [None, None]
        exf = [None, None]
        eof = [None, None]
        mkf = [None, None]
        n = 0
        m = 0
        rt = self.rpbrep.tensor
        for var in range(3):
            for k in E_VALID[var]:
                ms = m % 2
                m += 1
                SP.wait(mkf[ms])
                tm = self.dma(SP, mk[ms][:], self.emask[var, k], self.ds(f"bm{ms}"))
                td = None
                for h in range(8):
                    b = n % 2
                    n += 1
                    SP.wait(rawf[b])
                    for t in range(2):
                        a0 = E_ABASE[var] - 2 * k - t
                        off = ((h * 31 + a0) * 64) * 127 + 63
                        src = bass.AP(tensor=rt, offset=off, ap=[[126, 64], [64 * 127, 8], [1, 64]])
                        tr = self.dma(SP, raw[b][t * 64:(t + 1) * 64, :].rearrange("p (m c) -> p m c", c=64), src,
                                      self.ds(f"br{b}"))
                    ACT.wait(tr, exf[b])
                    te = ACT.sig(nc.scalar.activation(out=ex[b][:], in_=raw[b][:], func=AF.Exp))
                    rawf[b] = te
                    DVE.wait(te, tm, eof[b])
                    td = DVE.sig(nc.vector.tensor_tensor(out=eo[b][:], in0=ex[b][:], in1=mk[ms][:], op=ALU.mult))
                    exf[b] = td
                    SP.wait(td)
                    eof[b] = self.dma(SP, self.EALL[var, h, k], eo[b][:], self.ds(f"be{b}"))
                mkf[ms] = td


def mix0_attn_sample(self):
    nc = self.nc
    SP, PE, POOL = self.SP, self.PE, self.POOL
    with contextlib.ExitStack() as es:
        sb = lambda n, s, d: es.enter_context(self._sbuf(n, s, d))
        st = attn_setup(self, es, TS)
        qa = [sb(f"s_q{i}", [128, 8, TS], BF16) for i in range(2)]
        ka = [sb(f"s_k{i}", [128, 8, 1024], BF16) for i in range(2)]
        va = [sb(f"s_v{i}", [128, 8, 1024], BF16) for i in range(2)]
        ea = [sb(f"s_e{i}", [128, 8, TS], BF16) for i in range(2)]
        kc_ = sb("s_kc", [128, 8, 256], BF16)
        vc = sb("s_vc", [128, 2, 1024], BF16)
        tc1 = self.dma(SP, kc_[:], self.KCT.rearrange("h p t -> p h t"), self.ds("sc"))
        tc2 = self.dma(POOL, vc[:], self.cnav.rearrange("(kc p) d -> p kc d", p=128), self.ds("sw0"))
        PE.wait(tc1, tc2)
        blkfree = [None, None]
        efree = [None, None]
        ne = 0
        for b in range(8):
            var = 0 if b == 0 else (2 if b == 7 else 1)
            kr0 = min(max(8 * b - 4, 0), 48)
            sl = b % 2
            q0 = NPR + b * TS
            k0 = NPR + kr0 * 64
            SP.wait(blkfree[sl])
            self.dma(SP, qa[sl][:], self.QT[:, :, q0:q0 + TS].rearrange("h p t -> p h t"), self.ds(f"sl{sl}"))
            self.dma(SP, ka[sl][:], self.KT[:, :, k0:k0 + 1024].rearrange("h p t -> p h t"), self.ds(f"sl{sl}"))
            tl = self.dma(SP, va[sl][:], self.VS[k0:k0 + 1024, :].rearrange("(kc p) d -> p kc d", p=128),
                          self.ds(f"sl{sl}"))
            tlast = None
            for h in range(8):
                es_ = ne % 2
                ne += 1
                SP.wait(efree[es_])
                te = self.dma(SP, ea[es_][:], self.EALL[var, h].rearrange("k p q -> p k q"), self.ds(f"se{es_}"))
                chunks = []
                for k in E_VALID[var]:
                    chunks.append(dict(kq=[(ka[sl][:, h, k * 128:(k + 1) * 128], qa[sl][:, h, :])],
                                       v=va[sl][:, k, h * 128:(h + 1) * 128], e=ea[es_][:, k, :]))
                for kc in range(2):
                    chunks.append(dict(kq=[(kc_[:, h, kc * 128:(kc + 1) * 128], qa[sl][:, h, :])],
                                       v=vc[:, kc, h * 128:(h + 1) * 128]))
                self.DVE.wait(te)
                tlast = attn_block(self, st, TS, chunks, 128 ** -0.5, self.CT[h, :, q0:q0 + TS], [tl, te])
                efree[es_] = [tlast, (self.DVE, self.DVE.cnt)]
            blkfree[sl] = tlast
        self.barrier()


KB.ctx_prep = ctx_prep
KB.mix0_inproj = mix0_inproj
KB.mix0_pool = mix0_pool
KB.mix0_attn_prompt = mix0_attn_prompt
KB.mix0_ebuild = mix0_ebuild
KB.ebuild_body = ebuild_body
KB.mix0_attn_sample = mix0_attn_sample


def fakemod(self):
    nc = self.nc
    for t in (self.SC1, self.SH1):
        nc.vector.memset(t[:], 0.5)
    for t in (self.GSC, self.G2, self.B2):
        nc.vector.memset(t[:], 0.25)
    self.DVE.sig(nc.vector.memset(self.MOD[:], 0.1))
    self.barrier()


KB.fakemod = fakemod


def rms_fm(self, buf, psS, gvec, dst_fn, ob, obf, tk, tokbuf, sqt, sqf, rt):
    nc = self.nc
    SP, PE, ACT, DVE = self.SP, self.PE, self.ACT, self.DVE
    tS = None
    for c4 in range(4):
        b = c4 % 2
        ACT.wait(tokbuf[c4], sqf[b])
        tq = ACT.sig(nc.scalar.activation(out=sqt[b][:], in_=buf[:, c4, :], func=AF.Square))
        PE.wait(tq)
        if c4 == 0:
            PE.wait(rt.get("psfree"))
        tS = PE.sig(nc.tensor.matmul(psS[:], self.onesf[:], sqt[b][:], start=(c4 == 0), stop=(c4 == 3)))
        sqf[b] = tS
    DVE.wait(tS)
    tv = DVE.chain(nc.vector.tensor_scalar(out=rt["ms"][:], in0=psS[:], scalar1=1.0 / 512, scalar2=RMS_EPS,
                                           op0=ALU.mult, op1=ALU.add))
    rt["psfree"] = tv
    ACT.wait(tv)
    tsd = ACT.sig(nc.scalar.activation(out=rt["rstd"][:], in_=rt["ms"][:], func=AF.Sqrt))
    DVE.wait(tsd)
    DVE.chain(nc.vector.reciprocal(out=rt["rstd"][:], in_=rt["rstd"][:]))
    for c4 in range(4):
        b = c4 % 2
        DVE.wait(obf[b])
        to = DVE.sig(nc.vector.scalar_tensor_tensor(out=ob[b][:], in0=buf[:, c4, :], scalar=gvec[:, c4:c4 + 1],
                                                    in1=rt["rstd"][:], op0=ALU.mult, op1=ALU.mult))
        SP.wait(to)
        obf[b] = self.dma(SP, dst_fn(c4), ob[b][:], self.ds(f"dob{b}"))
    return to


def rope_evac(self, psA, psB, cs, sn, t1, t2, out_ap, n):
    nc = self.nc
    DVE = self.DVE
    DVE.chain(nc.vector.tensor_tensor(out=t1[0:64, 0:n], in0=psA[0:64, 0:n], in1=cs[0:64, 0:n], op=ALU.mult))
    DVE.chain(nc.vector.tensor_tensor(out=t2[0:64, 0:n], in0=psB[0:64, 0:n], in1=sn[0:64, 0:n], op=ALU.mult))
    return DVE.chain(nc.vector.tensor_tensor(out=out_ap, in0=t1[0:64, 0:n], in1=t2[0:64, 0:n], op=ALU.add))


def mla_down(self):
    nc = self.nc
    SP, PE, ACT, DVE, POOL = self.SP, self.PE, self.ACT, self.DVE, self.POOL
    with contextlib.ExitStack() as es:
        sb = lambda n, s, d: es.enter_context(self._sbuf(n, s, d))
        hT = [sb(f"d_h{i}", [128, DC, TS], BF16) for i in range(2)]
        wdn = sb("d_w", [128, 16, 1152], BF16)
        bufq = sb("d_bq", [128, 4, TS], F32)
        bufk = sb("d_bk", [128, 4, TS], F32)
        sqt = [sb(f"d_sq{i}", [128, TS], F32) for i in range(2)]
        rt = dict(ms=sb("d_ms", [128, TS], F32), rstd=sb("d_rs", [128, TS], F32))
        ob = [sb(f"d_ob{i}", [128, TS], BF16) for i in range(2)]
        cs = [sb(f"d_cs{i}", [64, TS], F32) for i in range(2)]
        sn = [sb(f"d_sn{i}", [64, TS], F32) for i in range(2)]
        t1 = sb("d_t1", [64, TS], F32)
        t2 = sb("d_t2", [64, TS], F32)
        ko = [sb(f"d_ko{i}", [64, TS], BF16) for i in range(2)]
        kvrow = sb("d_kvrow", [1, 512], F32)
        kvbc = sb("d_kvbc", [128, 512], F32)
        ta = sb("d_ta", [128, 512], F32)
        tb_ = sb("d_tb", [128, 64], F32)
        ss = sb("d_ss", [128, 2], F32)
        junk = sb("d_junk", [128, 512], F32)
        dsw = self.ds("sw0")
        src = self.mla_w_down[0]
        self.dma(POOL, wdn[:, :, 0:1088], src.rearrange("(kc p) n -> p kc n", p=128), dsw)
        pe_src = src[:, 1024:1088].rearrange("(kc p) (b2 b1 e) -> p kc b2 b1 e", p=128, b2=2, b1=2)
        pe_dst = wdn[:, :, 1088:1152].rearrange("p kc (b2 b1 e) -> p kc b2 b1 e", b2=2, b1=2)
        tw = None
        for b2 in range(2):
            for b1 in range(2):
                tw = self.dma(POOL, pe_dst[:, :, b2, 1 - b1, :], pe_src[:, :, b2, b1, :], dsw)
        tkr = self.dma(SP, kvrow[:], self.mla_kv_norm, self.ds("dkr"))
        PE.wait(tw, tkr)
        tp = PE.sig(nc.tensor.matmul(self.PS[7][:], self.onesf[0:1, :], kvrow[0:1, :], start=True, stop=True))
        DVE.wait(tp)
        tkb = DVE.chain(nc.vector.tensor_copy(out=kvbc[:], in_=self.PS[7][:]))
        PE.wait(tkb)
        hfree = [None, None]
        psf = [None] * 8
        obf = [None, None]
        sqf = [None, None]
        kof = [None, None]
        csf = [None, None]
        n = 0
        for tile in range(NTILE):
            sl = tile % 2
            tk = slice(tile * TS, (tile + 1) * TS)
            SP.wait(hfree[sl])
            th = self.dma(SP, hT[sl][:], self.HT[:, :, tk].rearrange("c p t -> p c t"), self.ds(f"dh{sl}"))
            tcs = None
            if tile > 0:
                SP.wait(csf[sl])
                self.dma(SP, cs[sl][:], self.ropec[:, (tile - 1) * TS:tile * TS], self.ds(f"dcs{sl}"))
                tcs = self.dma(SP, sn[sl][:], self.ropes[:, (tile - 1) * TS:tile * TS], self.ds(f"dcs{sl}"))
            PE.wait(th)
            for gi, (buf, col0, gvec, dstT, psS) in enumerate(((bufq, 0, self.QNG, self.CQT, self.PS[2]),
                                                              (bufk, 512, self.KVG, self.CKVT, self.PS[3]))):
                tokbuf = [None] * 4
                for c4 in range(4):
                    b = n % 2
                    n += 1
                    ps = self.PS[b]
                    PE.wait(psf[b])
                    for kc in range(16):
                        ins = nc.tensor.matmul(ps[:], wdn[:, kc, col0 + c4 * 128:col0 + (c4 + 1) * 128], hT[sl][:, kc, :],
                                               start=(kc == 0), stop=(kc == 15))
                    tpp = PE.sig(ins)
                    DVE.wait(tpp)
                    tokbuf[c4] = DVE.sig(nc.vector.tensor_copy(out=buf[:, c4, :], in_=ps[:]))
                    psf[b] = tokbuf[c4]
                rms_fm(self, buf, psS, gvec, lambda c4, dstT=dstT: dstT[c4, :, tk], ob, obf, tk, tokbuf, sqt, sqf, rt)
            kb = tile % 2
            PE.wait(psf[4], psf[5])
            for kc in range(16):
                ins = nc.tensor.matmul(self.PS[4][0:64, :], wdn[:, kc, 1024:1088], hT[sl][:, kc, :],
                                       start=(kc == 0), stop=(kc == 15))
            tA = PE.sig(ins)
            if tile > 0:
                for kc in range(16):
                    ins = nc.tensor.matmul(self.PS[5][0:64, :], wdn[:, kc, 1088:1152], hT[sl][:, kc, :],
                                           start=(kc == 0), stop=(kc == 15))
                tB = PE.sig(ins)
                DVE.wait(tA, tB, tcs, kof[kb])
                tko = rope_evac(self, self.PS[4], self.PS[5], cs[sl], sn[sl], t1, t2, ko[kb][0:64, :], TS)
                csf[sl] = tko
            else:
                DVE.wait(tA, kof[kb])
                tko = DVE.sig(nc.vector.tensor_copy(out=ko[kb][:], in_=self.PS[4][0:64, :]))
            psf[4] = tko
            psf[5] = tko
            SP.wait(tko)
            kof[kb] = self.dma(SP, self.KPET[:, tk], ko[kb][:], self.ds(f"dko{kb}"))
            tlast = tA
            if tile == 0:
                for s in range(4):
                    PE.wait(psf[6], psf[7])
                    for kc in range(16):
                        ins = nc.tensor.matmul(self.PS[6][:], hT[sl][:, kc, s * 128:(s + 1) * 128], wdn[:, kc, 512:1024],
                                               start=(kc == 0), stop=(kc == 15))
                    tA2 = PE.sig(ins)
                    for kc in range(16):
                        ins = nc.tensor.matmul(self.PS[7][:, 0:64], hT[sl][:, kc, s * 128:(s + 1) * 128],
                                               wdn[:, kc, 1024:1088], start=(kc == 0), stop=(kc == 15))
                    tB2 = PE.sig(ins)
                    tlast = tB2
                    ACT.wait(tA2)
                    tsq = ACT.sig(nc.scalar.activation(out=junk[:], in_=self.PS[6][:], func=AF.Square,
                                                       accum_out=ss[:, 0:1]))
                    DVE.wait(tsq)
                    tv = DVE.chain(nc.vector.tensor_scalar(out=ss[:, 1:2], in0=ss[:, 0:1], scalar1=1.0 / 512,
                                                           scalar2=RMS_EPS, op0=ALU.mult, op1=ALU.add))
                    ACT.wait(tv)
                    tsd = ACT.sig(nc.scalar.activation(out=ss[:, 0:1], in_=ss[:, 1:2], func=AF.Sqrt))
                    DVE.wait(tsd)
                    DVE.chain(nc.vector.reciprocal(out=ss[:, 1:2], in_=ss[:, 0:1]))
                    DVE.wait((self.ds("dta"), self.ds("dta").cnt))
                    tta = DVE.sig(nc.vector.scalar_tensor_tensor(out=ta[:], in0=self.PS[6][:], scalar=ss[:, 1:2],
                                                                 in1=kvbc[:], op0=ALU.mult, op1=ALU.mult))
                    psf[6] = tta
                    DVE.wait(tB2)
                    ttb = DVE.sig(nc.vector.tensor_copy(out=tb_[:], in_=self.PS[7][:, 0:64]))
                    psf[7] = ttb
                    SP.wait(tta, ttb)
                    self.dma(SP, self.onckv[s * 128:(s + 1) * 128, :], ta[:], self.ds("dta"))
                    self.dma(SP, self.onkpe[s * 128:(s + 1) * 128, :], tb_[:], self.ds("dta"))
            hfree[sl] = tlast
        self.barrier()


def mla_qproj(self):
    nc = self.nc
    SP, PE, ACT, DVE, POOL = self.SP, self.PE, self.ACT, self.DVE, self.POOL
    with contextlib.ExitStack() as es:
        sb = lambda n, s, d: es.enter_context(self._sbuf(n, s, d))
        wuq = sb("u_w", [128, 4, 3072 + 1024], BF16)
        cq = [sb(f"u_cq{i}", [128, 4, TS], BF16) for i in range(2)]
        cs = [sb(f"u_cs{i}", [64, TS], F32) for i in range(2)]
        sn = [sb(f"u_sn{i}", [64, TS], F32) for i in range(2)]
        t1 = sb("u_t1", [64, TS], F32)
        t2 = sb("u_t2", [64, TS], F32)
        ob = [sb(f"u_ob{i}", [128, TS], BF16) for i in range(2)]
        po = [sb(f"u_po{i}", [64, TS], BF16) for i in range(2)]
        dsw = self.ds("sw0")
        src = self.mla_w_uq[0]
        self.dma(POOL, wuq[:, :, 0:3072], src.rearrange("(kc p) n -> p kc n", p=128), dsw)
        tw = None
        for kc in range(4):
            sv = src[kc * 128:(kc + 1) * 128, :].rearrange("p (h x) -> p h x", x=192)[:, :, 128:192]
            sv = sv.rearrange("p h (b2 b1 e) -> p h b2 b1 e", b2=2, b1=2)
            dv = wuq[:, kc, 3072:4096].rearrange("p (h b2 b1 e) -> p h b2 b1 e", h=16, b2=2, b1=2)
            for b2 in range(2):
                for b1 in range(2):
                    tw = self.dma(POOL, dv[:, :, b2, 1 - b1, :], sv[:, :, b2, b1, :], dsw)
        PE.wait(tw)
        cfree = [None, None]
        csf = [None, None]
        psf = [None] * 6
        obf = [None, None]
        pof = [None, None]
        n = 0
        m = 0
        for tile in range(NTILE):
            sl = tile % 2
            tk = slice(tile * TS, (tile + 1) * TS)
            SP.wait(cfree[sl])
            tc_ = self.dma(SP, cq[sl][:], self.CQT[:, :, tk].rearrange("c p t -> p c t"), self.ds(f"uc{sl}"))
            tcs = None
            if tile > 0:
                SP.wait(csf[sl])
                self.dma(SP, cs[sl][:], self.ropec[:, (tile - 1) * TS:tile * TS], self.ds(f"ucs{sl}"))
                tcs = self.dma(SP, sn[sl][:], self.ropes[:, (tile - 1) * TS:tile * TS], self.ds(f"ucs{sl}"))
            PE.wait(tc_)
            tko = None
            for h in range(16):
                b = n % 2
                n += 1
                ps = self.PS[b]
                PE.wait(psf[b])
                for c4 in range(4):
                    ins = nc.tensor.matmul(ps[:], wuq[:, c4, h * 192:h * 192 + 128], cq[sl][:, c4, :],
                                           start=(c4 == 0), stop=(c4 == 3))
                tp = PE.sig(ins)
                ACT.wait(tp, obf[b])
                te = ACT.sig(nc.scalar.copy(out=ob[b][:], in_=ps[:]))
                psf[b] = te
                SP.wait(te)
                obf[b] = self.dma(SP, self.QNT[h, :, tk], ob[b][:], self.ds(f"uob{b}"))
                pb = m % 2
                m += 1
                psA, psB = self.PS[2 + 2 * pb], self.PS[3 + 2 * pb]
                PE.wait(psf[2 + 2 * pb], psf[3 + 2 * pb])
                for c4 in range(4):
                    ins = nc.tensor.matmul(psA[0:64, :], wuq[:, c4, h * 192 + 128:h * 192 + 192], cq[sl][:, c4, :],
                                           start=(c4 == 0), stop=(c4 == 3))
                tA = PE.sig(ins)
                if tile > 0:
                    for c4 in range(4):
                        ins = nc.tensor.matmul(psB[0:64, :], wuq[:, c4, 3072 + h * 64:3072 + (h + 1) * 64], cq[sl][:, c4, :],
                                               start=(c4 == 0), stop=(c4 == 3))
                    tB = PE.sig(ins)
                    DVE.wait(tA, tB, tcs, pof[pb])
                    tko = rope_evac(self, psA, psB, cs[sl], sn[sl], t1, t2, po[pb][0:64, :], TS)
                else:
                    DVE.wait(tA, pof[pb])
                    tko = DVE.sig(nc.vector.tensor_copy(out=po[pb][:], in_=psA[0:64, :]))
                psf[2 + 2 * pb] = tko
                psf[3 + 2 * pb] = tko
                SP.wait(tko)
                pof[pb] = self.dma(SP, self.QPET[h, :, tk], po[pb][:], self.ds(f"upo{pb}"))
            cfree[sl] = (PE, PE.cnt)
            csf[sl] = tko
        self.barrier()


def mla_attn(self):
    nc = self.nc
    SP, PE, ACT, DVE, POOL = self.SP, self.PE, self.ACT, self.DVE, self.POOL
    scale = 192 ** -0.5
    NK = NSA + 256
    NCH = NK // 128
    with contextlib.ExitStack() as es:
        sb = lambda n, s, d: es.enter_context(self._sbuf(n, s, d))
        st = attn_setup(self, es, TS)
        wukv = sb("v_w", [128, 4, 4096], BF16)
        ckvT = sb("v_ckv", [128, 4, NK], BF16)
        kpeT = sb("v_kpe", [64, NK], BF16)
        kn = sb("v_kn", [128, NK], BF16)
        vh = sb("v_vh", [128, NCH, 128], BF16)
        qn = [sb(f"v_qn{i}", [128, NSA], BF16) for i in range(2)]
        qp = [sb(f"v_qp{i}", [64, NSA], BF16) for i in range(2)]
        tw = self.dma(POOL, wukv[:], self.mla_w_ukv[0].rearrange("(kc p) n -> p kc n", p=128), self.ds("sw0"))
        psf = [None] * 4
        PE.wait(tw)

        def project(h, ckv, nkeys, kn_t, vh_t, ready):
            PE.wait(ready)
            t = None
            for k0 in range(0, nkeys, 512):
                w = min(512, nkeys - k0)
                PE.wait(psf[3])
                for c4 in range(4):
                    ins = nc.tensor.matmul(self.PS[3][:, 0:w], wukv[:, c4, h * 256:h * 256 + 128], ckv[:, c4, k0:k0 + w],
                                           start=(c4 == 0), stop=(c4 == 3))
                tp = PE.sig(ins)
                ACT.wait(tp)
                t = ACT.sig(nc.scalar.copy(out=kn_t[:, k0:k0 + w], in_=self.PS[3][:, 0:w]))
                psf[3] = t
            nch = nkeys // 128
            for g0 in range(0, nch, 4):
                gn = min(4, nch - g0)
                PE.wait(psf[3])
                for ci in range(gn):
                    ch = g0 + ci
                    for c4 in range(4):
                        ins = nc.tensor.matmul(self.PS[3][:, ci * 128:(ci + 1) * 128], ckv[:, c4, ch * 128:(ch + 1) * 128],
                                               wukv[:, c4, h * 256 + 128:h * 256 + 256], start=(c4 == 0), stop=(c4 == 3))
                tp = PE.sig(ins)
                ACT.wait(tp)
                t2_ = ACT.sig(nc.scalar.copy(out=vh_t[:, g0:g0 + gn, :].rearrange("p g d -> p (g d)"),
                                             in_=self.PS[3][:, 0:gn * 128]))
                psf[3] = t2_
                t = t2_
            return t

        self.dma(SP, ckvT[:], self.CKVT[:, :, NPR:NT + 256].rearrange("c p t -> p c t"), self.ds("vl"))
        tl = self.dma(SP, kpeT[:], self.KPET[:, NPR:NT + 256], self.ds("vl"))
        qfree = [None, None]
        for h in range(16):
            sl = h % 2
            SP.wait(qfree[sl])
            self.dma(SP, qn[sl][:], self.QNT[h, :, NPR:NT], self.ds(f"vq{sl}"))
            tq = self.dma(SP, qp[sl][:], self.QPET[h, :, NPR:NT], self.ds(f"vq{sl}"))
            tproj = project(h, ckvT, NK, kn, vh, tl)
            tlast = None
            for qb in range(NSA // TS):
                qs = slice(qb * TS, (qb + 1) * TS)
                chunks = [dict(kq=[(kn[:, c * 128:(c + 1) * 128], qn[sl][:, qs]),
                                   (kpeT[0:64, c * 128:(c + 1) * 128], qp[sl][0:64, qs])],
                               v=vh[:, c, :]) for c in range(NCH)]
                tlast = attn_block(self, st, TS, chunks, scale, self.CT[h, :, NPR + qb * TS:NPR + (qb + 1) * TS],
                                   [tq, tproj], dacc=True)
            qfree[sl] = tlast
        ckp = sb("v_ckp", [128, 4, 256], BF16)
        kpp = sb("v_kpp", [64, 256], BF16)
        qna = sb("v_qna", [128, 16, 256], BF16)
        qpa = sb("v_qpa", [64, 16, 256], BF16)
        knp = sb("v_knp", [128, 256], BF16)
        vhp = sb("v_vhp", [128, 2, 128], BF16)
        tprev = tlast
        for s in range(2):
            tk = slice(s * 256, (s + 1) * 256)
            SP.wait(tprev)
            self.dma(SP, ckp[:], self.CKVT[:, :, tk].rearrange("c p t -> p c t"), self.ds("vp"))
            self.dma(SP, kpp[:], self.KPET[:, tk], self.ds("vp"))
            self.dma(SP, qna[:], self.QNT[:, :, tk].rearrange("h p t -> p h t"), self.ds("vp"))
            tlp = self.dma(SP, qpa[:], self.QPET[:, :, tk].rearrange("h p t -> p h t"), self.ds("vp"))
            for h in range(16):
                tproj = project(h, ckp, 256, knp, vhp, tlp)
                chunks = [dict(kq=[(knp[:, c * 128:(c + 1) * 128], qna[:, h, :]),
                                   (kpp[0:64, c * 128:(c + 1) * 128], qpa[0:64, h, :])],
                               v=vhp[:, c, :]) for c in range(2)]
                tprev = attn_block(self, st, 256, chunks, scale, self.CT[h, :, tk], [tlp, tproj])
        self.barrier()


KB.mla_down = mla_down
KB.mla_qproj = mla_qproj
KB.mla_attn = mla_attn
```

```python
import contextlib
import numpy as np
import concourse.bass as bass
import concourse.mybir as mybir
from concourse.bass_utils import run_bass_kernel_spmd

F32 = mybir.dt.float32
BF16 = mybir.dt.bfloat16
AF = mybir.ActivationFunctionType
ALU = mybir.AluOpType

D = 2048
DC = 16
NPR = 512
NSA = 4096
NT = NPR + NSA
TS = 512
NTILE = NT // TS
DFF = 5632
FC = DFF // 128
DEPTH = 2
ALPHA = (2 * DEPTH) ** 0.25
LN_EPS = 1e-5
RMS_EPS = 1e-6
EPS_P = LN_EPS / (ALPHA * ALPHA)
GRID_W = 64
NKEY_MLA = NSA + 256
UPAD = 8
UTLEN = (256 + 2 * UPAD) * 2 + (NSA + 2 * UPAD)


IN_SPECS = {
    "xp": ("xp", [NPR, D]), "xs": ("xs", [NSA, D]), "cnak": ("cnak", [256, 1024]), "cnav": ("cnav", [256, 1024]),
    "cckv": ("cckv", [256, 512]), "ckpe": ("ckpe", [256, 64]), "cond": ("cond", [2, D]),
    "w_mod": ("w_mod", [DEPTH, D, 9 * D]), "b_mod": ("b_mod", [DEPTH, 9 * D]),
    "ln_g": ("ln_g", [DEPTH, 3, D]), "ln_b": ("ln_b", [DEPTH, 3, D]),
    "ffn_w1": ("ffn_w1", [DEPTH, 2, D, DFF]), "ffn_w3": ("ffn_w3", [DEPTH, 2, D, DFF]),
    "ffn_w2": ("ffn_w2", [DEPTH, 2, DFF, D]), "na_w_in": ("na_w_in", [1, D, 4096]),
    "mix0_w_out": ("mix0_w_out", [1, D, D]), "pool_w": ("pool_w", [1, 4, 256, 256]),
    "pool_scale": ("pool_scale", [1, 1024]), "mla_w_down": ("mla_w_down", [1, D, 1088]),
    "mla_q_norm": ("mla_q_norm", [1, 512]), "mla_w_uq": ("mla_w_uq", [1, 512, 3072]),
    "mla_kv_norm": ("mla_kv_norm", [1, 512]), "mla_w_ukv": ("mla_w_ukv", [1, 512, 4096]),
    "mla_w_out": ("mla_w_out", [1, D, D]),
    "identd": ("ident", [128, 128]), "rpbrep": ("rpbrep", [8 * 31 * 64 * 127]),
    "emask": ("emask", [3, 8, 128, 512]), "invcnt": ("invcnt", [4, 128, UTLEN]),
    "ropec": ("ropec", [64, NSA]), "ropes": ("ropes", [64, NSA]),
}


class Eng:
    def __init__(self, kb, eng, name):
        self.kb = kb
        self.e = eng
        self.name = name
        self.sem = kb.newsem("es_" + name)
        self.cnt = 0
        self.seen = {}

    def wait(self, *toks):
        for t in toks:
            if t is None:
                continue
            if isinstance(t, list):
                self.wait(*t)
                continue
            o, c = t
            if o is self:
                continue
            if self.seen.get(o, 0) >= c:
                continue
            self.e.wait_ge(o.sem, c)
            self.seen[o] = c

    def sig(self, inst):
        self.cnt += 1
        inst.then_inc(self.sem, 1)
        return (self, self.cnt)

    def chain(self, inst):
        t = self.sig(inst)
        self.e.wait_ge(self.sem, self.cnt)
        return t


class DSem:
    def __init__(self, kb, name):
        self.sem = kb.newsem("ds_" + name)
        self.cnt = 0


class KB:
    def __init__(self, dbg=(), plan=None):
        self.dbg = set(dbg)
        self.plan = plan
        nc = bass.Bass("TRN2", target_bir_lowering=False)
        self.nc = nc
        self._uid = 0
        _orig = nc.sbuf_tensor

        def _sbuf_unique(name, shape, dt):
            self._uid += 1
            return _orig(f"{name}_{self._uid}", shape, dt)
        self._sbuf = _sbuf_unique
        self.es = contextlib.ExitStack()
        self.dsems = {}
        self.outs = []
        self.ins = []
        self.pool_gate = None
        self.ds_pool = []
        self.ds_map = {}

    def __getattr__(self, name):
        if name in IN_SPECS:
            nm, shape = IN_SPECS[name]
            ap = self.nc.dram_tensor(nm, list(shape), F32, kind="ExternalInput").ap()
            self.__dict__[name] = ap
            self.ins.append(nm)
            return ap
        raise AttributeError(name)

    def newsem(self, name):
        return self.es.enter_context(self.nc.semaphore(name))

    def ds(self, name):
        if name.startswith("cv_") or name.startswith("sw"):
            if name not in self.dsems:
                self.dsems[name] = DSem(self, name)
            return self.dsems[name]
        if name not in self.ds_map:
            i = len(self.ds_map)
            if i >= len(self.ds_pool):
                d = DSem(self, f"pool{i}")
                self.ds_pool.append(d)
                self.dsems[f"pool{i}"] = d
            self.ds_map[name] = self.ds_pool[i]
        return self.ds_map[name]

    def dma(self, q, out, in_, ds, nogate=False):
        if q is self.POOL and self.pool_gate is not None and not nogate:
            q.wait(self.pool_gate)
        q.e.dma_start(out=out, in_=in_).then_inc(ds.sem, 16)
        ds.cnt += 16
        return (ds, ds.cnt)

    def dram_in(self, name, shape, dt=F32):
        return self.nc.dram_tensor(name, list(shape), dt, kind="ExternalInput").ap()

    def dram_out(self, name, shape, dt=F32):
        self.outs.append(name)
        return self.nc.dram_tensor(name, list(shape), dt, kind="ExternalOutput").ap()

    def scratch(self, name, shape, dt):
        if name in self.dbg:
            self.outs.append(name)
            return self.nc.dram_tensor(name, list(shape), dt, kind="ExternalOutput").ap()
        return self.nc.dram_tensor(name, list(shape), dt, kind="Internal").ap()

    def barrier(self):
        nc = self.nc
        for nm, d in self.dsems.items():
            if d.cnt > 0 and not nm.startswith("cv_"):
                self.SP.wait((d, d.cnt))
        toks = [(e, e.cnt) for e in (self.PE, self.ACT, self.DVE) if e.cnt > 0]
        self.SP.cnt += 1
        self.SP.e.sem_inc(self.SP.sem, 1)
        toks.append((self.SP, self.SP.cnt))
        for e in (self.PE, self.ACT, self.DVE, self.SP):
            e.wait(toks)
        self.pool_gate = toks
        self.ds_map = {}
        for e in (self.PE, self.ACT, self.DVE, self.SP):
            for nm, d in self.dsems.items():
                if not nm.startswith("cv_"):
                    e.seen[d] = d.cnt

    def build(self):
        nc = self.nc
        with self.es:
            self._build()
        return nc

    def _build(self):
        nc = self.nc
        es = self.es
        self.PE = Eng(self, nc.tensor, "pe")
        self.ACT = Eng(self, nc.scalar, "act")
        self.DVE = Eng(self, nc.vector, "dve")
        self.POOL = Eng(self, nc.gpsimd, "pool")
        self.SP = Eng(self, nc.sync, "sp")
        self.yp = self.dram_out("yp", [NPR, D])
        self.ys = self.dram_out("ys", [NSA, D])
        self.onk = self.dram_out("onk", [NPR, 1024])
        self.onv = self.dram_out("onv", [NPR, 1024])
        self.onckv = self.dram_out("onckv", [NPR, 512])
        self.onkpe = self.dram_out("onkpe", [NPR, 64])
        sc = self.scratch
        self.XT = sc("XT", [DC, 128, NT], F32)
        self.HT = sc("HT", [DC, 128, NT], BF16)
        self.W1B = [[sc(f"W1B{l}{j}", [22, 128, 16, 256], BF16) for j in range(2)] for l in range(DEPTH)]
        self.W3B = [[sc(f"W3B{l}{j}", [22, 128, 16, 256], BF16) for j in range(2)] for l in range(DEPTH)]
        self.W2B = [[sc(f"W2B{l}{j}", [DC, 128, FC, 128], BF16) for j in range(2)] for l in range(DEPTH)]
        self.WINB = sc("WINB", [24, 128, 16, 128], BF16)
        self.WO0B = sc("WO0B", [DC, 128, 16, 128], BF16)
        self.WO1B = sc("WO1B", [DC, 128, 16, 128], BF16)
        self.QT = sc("QT", [8, 128, NT], BF16)
        self.KT = sc("KT", [8, 128, NT], BF16)
        self.VS = sc("VS", [NT, 1024], BF16)
        self.UT = sc("UT", [8, 128, UTLEN], F32)
        self.CT = sc("CT", [DC, 128, NT], BF16)
        self.EALL = sc("EALL", [3, 8, 8, 128, 512], BF16)
        self.CQT = sc("CQT", [4, 128, NT], BF16)
        self.CKVT = sc("CKVT", [4, 128, NT + 256], BF16)
        self.KPET = sc("KPET", [64, NT + 256], BF16)
        self.QPET = sc("QPET", [16, 64, NT], BF16)
        self.QNT = sc("QNT", [16, 128, NT], BF16)
        self.KCT = sc("KCT", [8, 128, 256], BF16)
        self.WVB = sc("WVB", [128, 16, 1024], BF16)
        self.WKB = sc("WKB", [128, 16, 1024], BF16)

        sb = lambda n, s, d: es.enter_context(self._sbuf(n, s, d))
        self.ident = sb("ident_sb", [128, 128], F32)
        self.onesf = sb("onesf", [128, 128], F32)
        self.onesb = sb("onesb", [128, 128], BF16)
        self.bar_t = sb("bar_t", [1, 4], F32)
        self.MOD = sb("MOD", [128, DEPTH, 144, 2], F32)
        self.LG = sb("LG", [128, 6, DC], F32)
        self.LB = sb("LB", [128, 6, DC], F32)
        self.GSC = sb("GSC", [128, 2, 6, DC], F32)
        self.G2 = sb("G2", [128, 2, 6, DC], F32)
        self.B2 = sb("B2", [128, 2, 6, DC], F32)
        self.SC1 = sb("SC1", [128, 2, DC], F32)
        self.SH1 = sb("SH1", [128, 2, DC], F32)
        self.PSC = sb("PSC", [128, 8], F32)
        self.QNG = sb("QNG", [128, 4], F32)
        self.KVG = sb("KVG", [128, 4], F32)
        self.PS = [es.enter_context(nc.psum_tensor(f"ps{i}", [128, 512], F32)) for i in range(8)]

        plan = self.plan or ["const", "mod", "ctx", "init", "ffn00", "mix0", "ffn01", "ffn10", "mla", "ffn11", "final"]
        for ph in plan:
            if ph.startswith("ffn"):
                self.ffn_phase(int(ph[3]), int(ph[4]))
            else:
                getattr(self, {"const": "const_phase", "mod": "mod_phase", "conv": "conv_phase", "ctx": "ctx_prep",
                               "init": "init_phase", "mix0": "mix0_phase", "mla": "mla_phase",
                               "final": "final_phase"}.get(ph, ph))()

    def conv_phase(self):
        q = self.POOL
        self.cv = {}

        def conv_kn(dst, src, key, cols):
            ds = self.ds("cv_" + key)
            K = src.shape[0]
            for kc in range(K // 128):
                s = src[kc * 128:(kc + 1) * 128, :].rearrange("p (f c) -> p f c", c=cols)
                d_ = dst[:, :, kc, :].rearrange("f p c -> p f c")
                self.dma(q, d_, s, ds, nogate=True)
            self.cv[key] = (ds, ds.cnt)

        def conv_ffn(l, j):
            conv_kn(self.W1B[l][j], self.ffn_w1[l, j], f"f{l}{j}a", 256)
            conv_kn(self.W3B[l][j], self.ffn_w3[l, j], f"f{l}{j}a", 256)
            q.wait(self.cv[f"f{l}{j}a"])
            conv_kn(self.W2B[l][j], self.ffn_w2[l, j], f"f{l}{j}b", 128)
            q.wait(self.cv[f"f{l}{j}b"])

        conv_ffn(0, 0)
        ds = self.ds("cv_m0")
        for kc in range(16):
            rows = self.na_w_in[0, kc * 128:(kc + 1) * 128, :]
            self.dma(q, self.WINB[0:16, :, kc, :].rearrange("f p c -> p f c"),
                     rows[:, 0:2048].rearrange("p (f c) -> p f c", c=128), ds, nogate=True)
            self.dma(q, self.WINB[16:24, :, kc, :].rearrange("f p c -> p f c"),
                     rows[:, 3072:4096].rearrange("p (f c) -> p f c", c=128), ds, nogate=True)
        self.dma(q, self.WKB, self.na_w_in[0, :, 1024:2048].rearrange("(kc p) n -> p kc n", p=128), ds, nogate=True)
        self.dma(q, self.WVB, self.na_w_in[0, :, 2048:3072].rearrange("(kc p) n -> p kc n", p=128), ds, nogate=True)
        self.cv["m0"] = (ds, ds.cnt)
        q.wait(self.cv["m0"])
        conv_kn(self.WO0B, self.mix0_w_out[0], "m0o", 128)
        q.wait(self.cv["m0o"])
        conv_ffn(0, 1)
        conv_ffn(1, 0)
        conv_kn(self.WO1B, self.mla_w_out[0], "m1", 128)
        q.wait(self.cv["m1"])
        conv_ffn(1, 1)

    def load_fm(self, dst, src2d, nrows, ps, rows_tile):
        SP, PE, DVE = self.SP, self.PE, self.DVE
        t = self.dma(SP, rows_tile[0:nrows, :], src2d, self.ds("lfm"))
        PE.wait(t)
        tp = PE.sig(self.nc.tensor.transpose(out=ps[:, 0:nrows], in_=rows_tile[0:nrows, :],
                                              identity=self.ident[0:nrows, 0:nrows]))
        DVE.wait(tp)
        td = DVE.sig(self.nc.vector.tensor_copy(out=dst, in_=ps[:, 0:nrows]))
        SP.wait(td)
        PE.wait(td)
        return td

    def const_phase(self):
        nc = self.nc
        SP, PE, ACT, DVE = self.SP, self.PE, self.ACT, self.DVE
        t = self.dma(SP, self.ident[:], self.identd, self.ds("c0"))
        DVE.sig(nc.vector.memset(self.onesf[:], 1.0))
        tb = DVE.sig(nc.vector.memset(self.onesb[:], 1.0))
        PE.wait(t, tb)
        DVE.wait(t)
        ACT.wait(t, tb)
        with contextlib.ExitStack() as es:
            rows = es.enter_context(nc.sbuf_tensor("c_rows", [128, 128], F32))
            ps = self.PS[0]
            self.load_fm(self.LG[:].rearrange("p a c -> p (a c)"),
                         self.ln_g.rearrange("l s (c p) -> (l s c) p", p=128), 96, ps, rows)
            self.load_fm(self.LB[:].rearrange("p a c -> p (a c)"),
                         self.ln_b.rearrange("l s (c p) -> (l s c) p", p=128), 96, ps, rows)
            self.load_fm(self.PSC[:], self.pool_scale.rearrange("o (c p) -> (o c) p", p=128), 8, ps, rows)
            self.load_fm(self.QNG[:], self.mla_q_norm.rearrange("o (c p) -> (o c) p", p=128), 4, ps, rows)
            self.load_fm(self.KVG[:], self.mla_kv_norm.rearrange("o (c p) -> (o c) p", p=128), 4, ps, rows)
            self.barrier()

    def mod_phase(self):
        nc = self.nc
        SP, PE, ACT, DVE, POOL = self.SP, self.PE, self.ACT, self.DVE, self.POOL
        with contextlib.ExitStack() as es:
            sb = lambda n, s, d: es.enter_context(self._sbuf(n, s, d))
            rows = sb("m_rows", [128, 128], F32)
            cT = sb("m_cT", [128, 32], F32)
            scT = sb("m_scT", [128, 16, 2], BF16)
            bias = sb("m_bias", [128, DEPTH, 144], F32)
            wst = [sb(f"m_w{i}", [128, 16, 1024], BF16) for i in range(2)]
            ps = self.PS[0]
            tcT = self.load_fm(cT[:], self.cond.rearrange("j (c p) -> (j c) p", p=128), 32, ps, rows)
            ACT.wait(tcT)
            for l in range(DEPTH):
                bm = self.b_mod[l].rearrange("(n p) -> n p", p=128)
                self.load_fm(bias[:, l, 0:128], bm[0:128, :], 128, ps, rows)
                self.load_fm(bias[:, l, 128:144], bm[128:144, :], 16, ps, rows)
            ta = None
            for j in range(2):
                ta = ACT.sig(nc.scalar.activation(out=scT[:, :, j], in_=cT[:, j * 16:(j + 1) * 16], func=AF.Silu))
            PE.wait(ta)
            self.ebuild_body(es)
            free = [None, None]
            tps = []
            for l in range(DEPTH):
                psm = self.PS[1 + l]
                for st in range(18):
                    sl = (l * 18 + st) % 2
                    POOL.wait(free[sl])
                    src = self.w_mod[l, :, st * 1024:(st + 1) * 1024].rearrange("(kc p) n -> p kc n", p=128)
                    tl = self.dma(POOL, wst[sl][:], src, self.ds(f"sw{sl}"))
                    PE.wait(tl)
                    tp = None
                    for nn in range(8):
                        nch = st * 8 + nn
                        for kc in range(16):
                            ins = nc.tensor.matmul(psm[:, nch * 2:nch * 2 + 2], wst[sl][:, kc, nn * 128:(nn + 1) * 128],
                                                   scT[:, kc, :], start=(kc == 0), stop=(kc == 15))
                    tp = PE.sig(ins)
                    free[sl] = tp
                tps.append(tp)
            self.conv_phase()
            for l in range(DEPTH):
                psm = self.PS[1 + l]
                DVE.wait(tps[l])
                for j in range(2):
                    DVE.chain(nc.vector.tensor_tensor(
                        out=self.MOD[:, l, :, j], in0=psm[:, 0:288].rearrange("p (n j) -> p n j", j=2)[:, :, j],
                        in1=bias[:, l, :], op=ALU.add))
            C = DVE.chain
            for grp in range(2):
                def mv(l, v):
                    return self.MOD[:, l, v * 16:(v + 1) * 16, grp]
                C(nc.vector.tensor_scalar(out=self.SC1[:, grp, :], in0=mv(0, 1), scalar1=1.0, scalar2=None, op0=ALU.add))
                C(nc.vector.tensor_copy(out=self.SH1[:, grp, :], in_=mv(0, 0)))
                for l in range(DEPTH):
                    for s in range(3):
                        i = l * 3 + s
                        cgs = (0.5 if s != 1 else 1.0) / ALPHA
                        C(nc.vector.tensor_scalar(out=self.GSC[:, grp, i, :], in0=mv(l, 3 * s + 2), scalar1=cgs,
                                                  scalar2=None, op0=ALU.mult))
                        if s < 2:
                            nl, ns = l, s + 1
                        elif l + 1 < DEPTH:
                            nl, ns = l + 1, 0
                        else:
                            nl = None
                        if nl is None:
                            C(nc.vector.tensor_copy(out=self.G2[:, grp, i, :], in_=self.LG[:, i, :]))
                            C(nc.vector.tensor_copy(out=self.B2[:, grp, i, :], in_=self.LB[:, i, :]))
                        else:
                            C(nc.vector.scalar_tensor_tensor(out=self.G2[:, grp, i, :], in0=mv(nl, 3 * ns + 1), scalar=1.0,
                                                             in1=self.LG[:, i, :], op0=ALU.add, op1=ALU.mult))
                            C(nc.vector.scalar_tensor_tensor(out=self.B2[:, grp, i, :], in0=mv(nl, 3 * ns + 1), scalar=1.0,
                                                             in1=self.LB[:, i, :], op0=ALU.add, op1=ALU.mult))
                            C(nc.vector.tensor_tensor(out=self.B2[:, grp, i, :], in0=self.B2[:, grp, i, :],
                                                      in1=mv(nl, 3 * ns), op=ALU.add))
            tdv = self.DVE.sig(nc.vector.memset(rows[:], 0.0))
            if "DBGMOD" in self.dbg:
                dd = self.dram_out("DBGMOD", [5, 128, 192])
                SP.wait(tdv)
                for i_, t_ in enumerate((self.GSC, self.G2, self.B2)):
                    self.dma(SP, dd[i_], t_[:].rearrange("p a b c -> p (a b c)"), self.ds("dbgm"))
                self.dma(SP, dd[3][:, 0:96], self.LG[:].rearrange("p a c -> p (a c)"), self.ds("dbgm"))
                self.dma(SP, dd[4][:, 0:96], self.LB[:].rearrange("p a c -> p (a c)"), self.ds("dbgm"))
            self.barrier()

    def init_phase(self):
        nc = self.nc
        SP, PE, ACT, DVE = self.SP, self.PE, self.ACT, self.DVE
        with contextlib.ExitStack() as es:
            sb = lambda n, s, d: es.enter_context(self._sbuf(n, s, d))
            xin = [sb(f"i_x{i}", [128, 4, D], F32) for i in range(2)]
            xo = [sb(f"i_xo{i}", [128, TS], F32) for i in range(2)]
            ho = [sb(f"i_ho{i}", [128, TS], BF16) for i in range(2)]
            xin_free = [None, None]
            xo_free = [None, None]
            ho_free = [None, None]
            ps_free = [None, None]
            n = 0
            for tile in range(NTILE):
                grp = 0 if tile == 0 else 1
                sl = tile % 2
                src = self.xp if tile == 0 else self.xs[(tile - 1) * TS:tile * TS, :]
                SP.wait(xin_free[sl])
                tl = self.dma(SP, xin[sl][:], src.rearrange("(s p) d -> p s d", p=128), self.ds(f"ix{sl}"))
                PE.wait(tl)
                for c in range(DC):
                    b = n % 2
                    n += 1
                    ps = self.PS[b]
                    PE.wait(ps_free[b])
                    for s in range(4):
                        ins = nc.tensor.transpose(out=ps[:, s * 128:(s + 1) * 128], in_=xin[sl][:, s, c * 128:(c + 1) * 128],
                                                  identity=self.ident[:])
                    tp = PE.sig(ins)
                    DVE.wait(tp, xo_free[b])
                    t1 = DVE.sig(nc.vector.tensor_copy(out=xo[b][:], in_=ps[:]))
                    ACT.wait(t1, ho_free[b])
                    t2 = ACT.sig(nc.scalar.activation(out=ho[b][:], in_=xo[b][:], func=AF.Identity,
                                                      scale=self.SC1[:, grp, c:c + 1], bias=self.SH1[:, grp, c:c + 1]))
                    ps_free[b] = t1
                    SP.wait(t1, t2)
                    xo_free[b] = self.dma(SP, self.XT[c, :, tile * TS:(tile + 1) * TS], xo[b][:], self.ds(f"ixo{b}"))
                    ho_free[b] = self.dma(SP, self.HT[c, :, tile * TS:(tile + 1) * TS], ho[b][:], self.ds(f"iho{b}"))
                xin_free[sl] = tp
            self.barrier()

    def epi_alloc(self, es):
        nc = self.nc
        sb = lambda n, s, d: es.enter_context(self._sbuf(n, s, d))
        ep = {}
        ep["w2"] = [sb(f"e_w2{i}", [128, FC, 128], BF16) for i in range(2)]
        ep["xres"] = [sb(f"e_xr{i}", [128, TS], F32) for i in range(2)]
        ep["v"] = sb("e_v", [128, DC, TS], F32)
        ep["sq"] = [sb(f"e_sq{i}", [128, TS], F32) for i in range(2)]
        ep["mean"] = sb("e_mean", [128, TS], F32)
        ep["var"] = sb("e_var", [128, TS], F32)
        ep["rstd"] = sb("e_rstd", [128, TS], F32)
        ep["nmr"] = sb("e_nmr", [128, TS], F32)
        ep["xn"] = [sb(f"e_xn{i}", [128, TS], F32) for i in range(2)]
        ep["xo"] = [sb(f"e_xo{i}", [128, TS], F32) for i in range(2)]
        ep["ho"] = [sb(f"e_ho{i}", [128, TS], BF16) for i in range(2)]
        ep["accv"] = sb("e_accv", [128, TS], F32)
        ep["accq"] = sb("e_accq", [128, TS], F32)
        ep["free"] = {}
        ep["pending"] = []
        return ep

    def down_epilogue(self, ep, tile, nk, rhs_fn, wscr, li, rhs_ready, psb=4, wready=None, prefetch=None):
        nc = self.nc
        SP, PE, ACT, DVE = self.SP, self.PE, self.ACT, self.DVE
        grp = 0 if tile == 0 else 1
        fr = ep["free"]
        tk = slice(tile * TS, (tile + 1) * TS)
        psY = [self.PS[psb], self.PS[psb + 1]]
        psS1, psS2 = self.PS[psb + 2], self.PS[psb + 3]
        v = ep["v"]
        tokV = [None] * DC
        tokSq = [None] * DC

        accv, accq = ep["accv"], ep["accq"]

        def accum(dc):
            b = dc % 2
            if dc == 0:
                DVE.wait(fr.get("acc"))
                DVE.chain(nc.vector.tensor_copy(out=accv[:], in_=v[:, 0, :]))
                DVE.wait(tokSq[dc])
                t = DVE.chain(nc.vector.tensor_copy(out=accq[:], in_=ep["sq"][b][:]))
            else:
                DVE.chain(nc.vector.tensor_tensor(out=accv[:], in0=accv[:], in1=v[:, dc, :], op=ALU.add))
                DVE.wait(tokSq[dc])
                t = DVE.chain(nc.vector.tensor_tensor(out=accq[:], in0=accq[:], in1=ep["sq"][b][:], op=ALU.add))
            fr[f"sq{b}"] = t
            return t

        PE.wait(rhs_ready)
        SP.wait(wready)
        tS = None
        pend = ep["pending"]

        def issue_loads(dc):
            b_ = dc % 2
            SP.wait(fr.get(f"w2{b_}"))
            tw_ = self.dma(SP, ep["w2"][b_][:, 0:nk, :], wscr[dc], self.ds(f"ew2{b_}"))
            SP.wait(fr.get(f"xr{b_}"))
            tx_ = self.dma(SP, ep["xres"][b_][:], self.XT[dc, :, tk], self.ds(f"exr{b_}"))
            return tw_, tx_

        ld = {0: issue_loads(0)}
        for dc in range(DC):
            b = dc % 2
            if dc + 1 < DC:
                ld[dc + 1] = issue_loads(dc + 1)
            for _ in range(2 if dc == 0 else 1):
                if pend:
                    pend.pop(0)()
            tw, tx = ld.pop(dc)
            PE.wait(tw, fr.get(f"psY{b}"))
            for k in range(nk):
                ins = nc.tensor.matmul(psY[b][:], ep["w2"][b][:, k, :], rhs_fn(k), start=(k == 0), stop=(k == nk - 1))
            tY = PE.sig(ins)
            fr[f"w2{b}"] = tY
            DVE.wait(tY, tx)
            tokV[dc] = DVE.sig(nc.vector.scalar_tensor_tensor(
                out=v[:, dc, :], in0=psY[b][:], scalar=self.GSC[:, grp, li, dc:dc + 1], in1=ep["xres"][b][:],
                op0=ALU.mult, op1=ALU.add))
            fr[f"psY{b}"] = tokV[dc]
            fr[f"xr{b}"] = tokV[dc]
            ACT.wait(tokV[dc], fr.get(f"sq{b}"))
            tokSq[dc] = ACT.sig(nc.scalar.activation(out=ep["sq"][b][:], in_=v[:, dc, :], func=AF.Square))
            if dc >= 1:
                accum(dc - 1)
        tacc = accum(DC - 1)
        PE.wait(tacc, fr.get("psS"))
        nc.tensor.matmul(psS1[:], self.onesf[:], accv[:], start=True, stop=True)
        tS = PE.sig(nc.tensor.matmul(psS2[:], self.onesf[:], accq[:], start=True, stop=True))
        fr["acc"] = tS
        if prefetch is not None:
            prefetch()
        def fin_stats():
            DVE.wait(tS)
            invd = 1.0 / D
            C = DVE.chain
            C(nc.vector.tensor_scalar(out=ep["mean"][:], in0=psS1[:], scalar1=invd, scalar2=None, op0=ALU.mult))
            C(nc.vector.tensor_tensor(out=ep["nmr"][:], in0=ep["mean"][:], in1=ep["mean"][:], op=ALU.mult))
            C(nc.vector.scalar_tensor_tensor(out=ep["var"][:], in0=psS2[:], scalar=invd, in1=ep["nmr"][:],
                                             op0=ALU.mult, op1=ALU.subtract))
            tvar = C(nc.vector.tensor_scalar(out=ep["var"][:], in0=ep["var"][:], scalar1=EPS_P, scalar2=None, op0=ALU.add))
            ACT.wait(tvar)
            tsd = ACT.sig(nc.scalar.activation(out=ep["rstd"][:], in_=ep["var"][:], func=AF.Sqrt))
            DVE.wait(tsd)
            C(nc.vector.reciprocal(out=ep["rstd"][:], in_=ep["rstd"][:]))
            tR = C(nc.vector.scalar_tensor_tensor(out=ep["nmr"][:], in0=ep["mean"][:], scalar=-1.0, in1=ep["rstd"][:],
                                                  op0=ALU.mult, op1=ALU.mult))
            fr["psS"] = tR

        def fin_dc(dc):
            b = dc % 2
            DVE.wait(fr.get(f"xn{b}"))
            DVE.chain(nc.vector.tensor_tensor(out=ep["xn"][b][:], in0=v[:, dc, :], in1=ep["rstd"][:], op=ALU.mult))
            tn = DVE.chain(nc.vector.tensor_tensor(out=ep["xn"][b][:], in0=ep["xn"][b][:], in1=ep["nmr"][:], op=ALU.add))
            ACT.wait(tn, fr.get(f"xo{b}"))
            ta = ACT.sig(nc.scalar.activation(out=ep["xo"][b][:], in_=ep["xn"][b][:], func=AF.Identity,
                                              scale=self.LG[:, li, dc:dc + 1], bias=self.LB[:, li, dc:dc + 1]))
            DVE.wait(fr.get(f"ho{b}"))
            th = DVE.sig(nc.vector.tensor_scalar(out=ep["ho"][b][:], in0=ep["xn"][b][:],
                                                 scalar1=self.G2[:, grp, li, dc:dc + 1],
                                                 scalar2=self.B2[:, grp, li, dc:dc + 1], op0=ALU.mult, op1=ALU.add))
            fr[f"xn{b}"] = [ta, th]
            SP.wait(ta)
            fr[f"xo{b}"] = self.dma(SP, self.XT[dc, :, tk], ep["xo"][b][:], self.ds(f"exo{b}"))
            SP.wait(th)
            fr[f"ho{b}"] = self.dma(SP, self.HT[dc, :, tk], ep["ho"][b][:], self.ds(f"eho{b}"))

        pend.append(fin_stats)
        for dc in range(DC):
            pend.append(lambda dc=dc: fin_dc(dc))

    def epi_flush(self, ep):
        while ep["pending"]:
            ep["pending"].pop(0)()

    def ffn_phase(self, l, j):
        nc = self.nc
        SP, PE, ACT, DVE = self.SP, self.PE, self.ACT, self.DVE
        li = l * 3 + (0 if j == 0 else 2)
        W1, W3, W2 = self.W1B[l][j], self.W3B[l][j], self.W2B[l][j]
        cvt = self.cv[f"f{l}{j}a"]
        with contextlib.ExitStack() as es:
            sb = lambda n, s, d: es.enter_context(self._sbuf(n, s, d))
            hT = sb("f_hT", [128, DC, TS], BF16)
            g = sb("f_g", [128, FC, TS], BF16)
            w1s = [sb(f"f_w1{i}", [128, 16, 256], BF16) for i in range(2)]
            w3s = [sb(f"f_w3{i}", [128, 16, 256], BF16) for i in range(2)]
            sg = [sb(f"f_sg{i}", [128, TS], F32) for i in range(2)]
            ep = self.epi_alloc(es)
            SP.wait(cvt)
            wfree = [None, None]
            ps1f = [None, None]
            ps3f = [None, None]
            sgf = [None, None]
            stt = {"hfree": None}
            pre = {}

            def issue_h(tile):
                SP.wait(stt["hfree"])
                return self.dma(SP, hT[:], self.HT[:, :, tile * TS:(tile + 1) * TS].rearrange("c p t -> p c t"),
                                self.ds("fh"))

            def issue_w(fp):
                sl = fp % 2
                SP.wait(wfree[sl])
                self.dma(SP, w1s[sl][:], W1[fp], self.ds(f"fw{sl}"))
                return self.dma(SP, w3s[sl][:], W3[fp], self.ds(f"fw{sl}"))

            for tile in range(NTILE):
                th = pre.pop("h", None) or issue_h(tile)
                PE.wait(th)
                tG = None
                for fp in range(22):
                    sl = fp % 2
                    tw = pre.pop(fp, None) or issue_w(fp)
                    for _ in range(2 if fp == 0 else 1):
                        if ep["pending"]:
                            ep["pending"].pop(0)()
                    PE.wait(tw)
                    for f2 in range(2):
                        f = fp * 2 + f2
                        b = f % 2
                        ps1, ps3 = self.PS[b], self.PS[2 + b]
                        PE.wait(ps1f[b])
                        for kc in range(16):
                            ins = nc.tensor.matmul(ps1[:], w1s[sl][:, kc, f2 * 128:(f2 + 1) * 128], hT[:, kc, :],
                                                   start=(kc == 0), stop=(kc == 15))
                        t1 = PE.sig(ins)
                        PE.wait(ps3f[b])
                        for kc in range(16):
                            ins = nc.tensor.matmul(ps3[:], w3s[sl][:, kc, f2 * 128:(f2 + 1) * 128], hT[:, kc, :],
                                                   start=(kc == 0), stop=(kc == 15))
                        t3 = PE.sig(ins)
                        ACT.wait(t1, sgf[b])
                        ts_ = ACT.sig(nc.scalar.activation(out=sg[b][:], in_=ps1[:], func=AF.Silu))
                        ps1f[b] = ts_
                        DVE.wait(ts_, t3)
                        tG = DVE.sig(nc.vector.tensor_tensor(out=g[:, f, :], in0=sg[b][:], in1=ps3[:], op=ALU.mult))
                        ps3f[b] = tG
                        sgf[b] = tG
                    wfree[sl] = t3
                stt["hfree"] = t3

                def prefetch(tile=tile):
                    if tile + 1 < NTILE:
                        pre["h"] = issue_h(tile + 1)
                        pre[0] = issue_w(0)
                        pre[1] = issue_w(1)
                self.down_epilogue(ep, tile, FC, lambda k: g[:, k, :], W2, li, tG, psb=4,
                                   wready=self.cv[f"f{l}{j}b"], prefetch=prefetch)
            self.epi_flush(ep)
            self.barrier()

    def outproj_phase(self, src, wscr, li, cvt):
        nc = self.nc
        SP, PE = self.SP, self.PE
        with contextlib.ExitStack() as es:
            sb = lambda n, s, d: es.enter_context(self._sbuf(n, s, d))
            cT = [sb(f"o_c{i}", [128, DC, TS], BF16) for i in range(2)]
            ep = self.epi_alloc(es)
            SP.wait(cvt)
            cfree = [None, None]
            pre = {}

            def issue_c(tile):
                sl = tile % 2
                SP.wait(cfree[sl])
                return self.dma(SP, cT[sl][:], src[:, :, tile * TS:(tile + 1) * TS].rearrange("c p t -> p c t"),
                                self.ds(f"oc{sl}"))

            for tile in range(NTILE):
                sl = tile % 2
                tl = pre.pop("c", None) or issue_c(tile)

                def prefetch(tile=tile):
                    if tile + 1 < NTILE:
                        pre["c"] = issue_c(tile + 1)
                self.down_epilogue(ep, tile, 16, lambda k, sl=sl: cT[sl][:, k, :], wscr, li, tl, psb=4, prefetch=prefetch)
                cfree[sl] = (self.PE, self.PE.cnt)
            self.epi_flush(ep)
            self.barrier()

    def mix0_phase(self):
        self.mix0_inproj()
        self.mix0_pool()
        self.mix0_attn_prompt()
        self.mix0_attn_sample()
        self.outproj_phase(self.CT, self.WO0B, 1, self.cv["m0o"])

    def mla_phase(self):
        self.mla_down()
        self.mla_qproj()
        self.mla_attn()
        self.outproj_phase(self.CT, self.WO1B, 4, self.cv["m1"])

    def final_phase(self):
        nc = self.nc
        SP, PE, ACT, DVE = self.SP, self.PE, self.ACT, self.DVE
        with contextlib.ExitStack() as es:
            sb = lambda n, s, d: es.enter_context(self._sbuf(n, s, d))
            xt = [sb(f"z_x{i}", [128, DC, TS], F32) for i in range(2)]
            yo = [sb(f"z_y{i}", [128, D], F32) for i in range(2)]
            xfree = [None, None]
            yfree = [None, None]
            psf = [None] * 4
            n = 0
            m = 0
            for tile in range(NTILE):
                sl = tile % 2
                tk = slice(tile * TS, (tile + 1) * TS)
                SP.wait(xfree[sl])
                tl = self.dma(SP, xt[sl][:], self.XT[:, :, tk].rearrange("c p t -> p c t"), self.ds(f"zx{sl}"))
                PE.wait(tl)
                for s in range(4):
                    yb = m % 2
                    m += 1
                    tlast = []
                    for q4 in range(4):
                        b = n % 4
                        n += 1
                        ps = self.PS[b]
                        PE.wait(psf[b])
                        for cc in range(4):
                            c = q4 * 4 + cc
                            ins = nc.tensor.transpose(out=ps[:, cc * 128:(cc + 1) * 128],
                                                      in_=xt[sl][:, c, s * 128:(s + 1) * 128], identity=self.ident[:])
                        tp = PE.sig(ins)
                        eng = DVE if q4 % 2 == 0 else ACT
                        eng.wait(tp, yfree[yb])
                        if eng is DVE:
                            tcp = DVE.sig(nc.vector.tensor_copy(out=yo[yb][:, q4 * 512:(q4 + 1) * 512], in_=ps[:]))
                        else:
                            tcp = ACT.sig(nc.scalar.copy(out=yo[yb][:, q4 * 512:(q4 + 1) * 512], in_=ps[:]))
                        psf[b] = tcp
                        tlast.append(tcp)
                    SP.wait(tlast)
                    if tile == 0:
                        dst = self.yp[s * 128:(s + 1) * 128, :]
                    else:
                        r0 = (tile - 1) * TS + s * 128
                        dst = self.ys[r0:r0 + 128, :]
                    yfree[yb] = self.dma(SP, dst, yo[yb][:], self.ds(f"zy{yb}"))
                xfree[sl] = tp
            self.barrier()


def _consts():
    c = {}
    c["ident"] = np.eye(128, dtype=np.float32)
    m = np.zeros((3, 8, 128, 512), np.float32)
    cc = np.arange(64)
    cs = np.clip(cc - 8, 0, 48)
    jj = np.arange(64)
    colv = ((jj[:, None] >= cs[None, :]) & (jj[:, None] < cs[None, :] + 16)).astype(np.float32)
    base = [(0, 0), (20, 24), (48, 56)]
    for var in range(3):
        i0, r0 = base[var]
        for k in range(8):
            for t in range(2):
                ia = i0 + 2 * k + t
                for mm in range(8):
                    r = r0 + mm
                    rs = min(max(r - 4, 0), 56)
                    if rs <= ia < rs + 8:
                        m[var, k, t * 64:(t + 1) * 64, mm * 64:(mm + 1) * 64] = colv
    c["emask"] = m
    inv = np.ones((4, UTLEN), np.float32)
    segs = [(0, 256), (272, 256), (544, NSA)]
    for g, w in enumerate((2, 4, 8, 16)):
        for off, L in segs:
            t = np.arange(L)
            cnt = np.minimum(t + w // 2, L) - np.maximum(t - w // 2, 0)
            inv[g, off + UPAD:off + UPAD + L] = (1.0 / cnt.astype(np.float32)).astype(np.float32)
    c["invcnt"] = np.ascontiguousarray(np.broadcast_to(inv[:, None, :], (4, 128, UTLEN)))
    t = np.arange(NSA)
    invf = (10000.0 ** (-np.arange(0, 32, 2, dtype=np.float32) / 32)).astype(np.float32)
    ang_row = (t // GRID_W).astype(np.float32)[None, :] * invf[:, None]
    ang_col = (t % GRID_W).astype(np.float32)[None, :] * invf[:, None]
    cosr, sinr = np.cos(ang_row), np.sin(ang_row)
    cosc, sinc = np.cos(ang_col), np.sin(ang_col)
    c["ropec"] = np.concatenate([cosr, cosr, cosc, cosc], 0).astype(np.float32)
    c["ropes"] = np.concatenate([-sinr, sinr, -sinc, sinc], 0).astype(np.float32)
    return c


_CONSTS = None


def make_in_maps(inp, cores=range(8)):
    global _CONSTS
    if _CONSTS is None:
        _CONSTS = _consts()
    f = lambda a: np.ascontiguousarray(np.asarray(a, dtype=np.float32))
    shared = {k: f(inp[k]) for k in ("w_mod", "b_mod", "ln_g", "ln_b", "ffn_w1", "ffn_w3", "ffn_w2", "na_w_in",
                                     "mix0_w_out", "pool_w", "pool_scale", "mla_w_down", "mla_q_norm", "mla_w_uq",
                                     "mla_kv_norm", "mla_w_ukv", "mla_w_out")}
    shared.update(_CONSTS)
    rpb = f(inp["na_rpb"])[0]
    rr = np.zeros((8, 31, 127), np.float32)
    rr[:, 8:23, 48:79] = rpb[:, ::-1, ::-1]
    shared["rpbrep"] = np.ascontiguousarray(np.broadcast_to(rr[:, :, None, :], (8, 31, 64, 127))).reshape(-1)
    maps = []
    for i in cores:
        m = dict(shared)
        m["xp"] = f(inp["x_prompt"][2 * i:2 * i + 2]).reshape(NPR, D)
        m["xs"] = f(inp["x_sample"][i])
        m["cnak"] = f(inp["cache_na_k"][i, 0]).reshape(256, 1024)
        m["cnav"] = f(inp["cache_na_v"][i, 0]).reshape(256, 1024)
        m["cckv"] = f(inp["cache_mla_ckv"][i, 0])
        m["ckpe"] = f(inp["cache_mla_kpe"][i, 0])
        m["cond"] = np.ascontiguousarray(np.stack([f(inp["c_ctx"]), f(inp["c"][i])], 0))
        maps.append(m)
    return maps


_NC = None
_KBO = None


def kernel(**inputs):
    global _NC, _KBO
    if _NC is None:
        _KBO = KB()
        _NC = _KBO.build()
    maps = make_in_maps(inputs)
    maps = [{k: m[k] for k in _KBO.ins} for m in maps]
    res = run_bass_kernel_spmd(_NC, maps, core_ids=list(range(8)))
    B, S = 16, 256
    y_prompt = np.zeros((B, S, D), np.float32)
    y_sample = np.zeros((8, NSA, D), np.float32)
    nk = np.zeros((B, 1, S, 8, 128), np.float32)
    nv = np.zeros((B, 1, S, 8, 128), np.float32)
    nckv = np.zeros((B, 1, S, 512), np.float32)
    nkpe = np.zeros((B, 1, S, 64), np.float32)
    for i, r in enumerate(res.results):
        y_prompt[2 * i:2 * i + 2] = np.asarray(r["yp"]).reshape(2, S, D)
        y_sample[i] = np.asarray(r["ys"])
        nk[2 * i:2 * i + 2, 0] = np.asarray(r["onk"]).reshape(2, S, 8, 128)
        nv[2 * i:2 * i + 2, 0] = np.asarray(r["onv"]).reshape(2, S, 8, 128)
        nckv[2 * i:2 * i + 2, 0] = np.asarray(r["onckv"]).reshape(2, S, 512)
        nkpe[2 * i:2 * i + 2, 0] = np.asarray(r["onkpe"]).reshape(2, S, 64)
    return (y_prompt, y_sample, nk, nv, nckv, nkpe)


def _units():
    u = [(UPAD, 256, 0), (272 + UPAD, 256, 256)]
    for i in range(NSA // TS):
        u.append((544 + UPAD + i * TS, TS, NPR + i * TS))
    return u


def ctx_prep(self):
    nc = self.nc
    SP, PE, ACT, DVE = self.SP, self.PE, self.ACT, self.DVE
    with contextlib.ExitStack() as es:
        sb = lambda n, s, d: es.enter_context(self._sbuf(n, s, d))
        src = sb("cp_src", [128, 2, 1024], F32)
        ob = sb("cp_ob", [128, 16, 256], BF16)
        jobs = [(self.cnak, 1024, "nak"), (self.cckv, 512, "ckv"), (self.ckpe, 64, "kpe")]
        for (dr, width, nm) in jobs:
            t = self.dma(SP, src[:, :, 0:width], dr.rearrange("(kc p) d -> p kc d", p=128), self.ds("cp"))
            PE.wait(t)
            nch = (width + 127) // 128
            for c in range(nch):
                w = min(128, width - c * 128)
                ps = self.PS[c % 2]
                for kc in range(2):
                    ins = nc.tensor.transpose(out=ps[0:w, kc * 128:(kc + 1) * 128], in_=src[:, kc, c * 128:c * 128 + w],
                                              identity=self.ident[:])
                tp = PE.sig(ins)
                DVE.wait(tp)
                td = DVE.sig(nc.vector.tensor_copy(out=ob[0:w, c, :], in_=ps[0:w, 0:256]))
                PE.wait(td)
            SP.wait(td)
            if nm == "nak":
                self.dma(SP, self.KCT.rearrange("h p t -> p h t"), ob[:, 0:8, :], self.ds("cp2"))
            elif nm == "ckv":
                self.dma(SP, self.CKVT[:, :, NT:NT + 256].rearrange("c p t -> p c t"), ob[:, 0:4, :], self.ds("cp2"))
            else:
                self.dma(SP, self.KPET[:, NT:NT + 256], ob[0:64, 0, :], self.ds("cp2"))
            SP.wait((self.ds("cp2"), self.ds("cp2").cnt))
        self.barrier()


class AttnState:
    pass


def attn_setup(self, es, nqmax):
    nc = self.nc
    st = AttnState()
    sb = lambda n, s, d: es.enter_context(self._sbuf(n, s, d))
    st.pT = [sb(f"a_pT{i}", [128, nqmax], BF16) for i in range(3)]
    st.rec = [sb(f"a_rec{i}", [128, nqmax], F32) for i in range(2)]
    st.acc = [sb(f"a_acc{i}", [128, nqmax], F32) for i in range(2)]
    st.accfree = [None] * 2
    st.o = [sb(f"a_o{i}", [128, nqmax], BF16) for i in range(2)]
    st.sfree = [None] * 3
    st.pfree = [None] * 3
    st.ofree = [None] * 2
    st.recfree = [None] * 2
    st.ostore = [None] * 2
    st.nS = 0
    st.nO = 0
    return st


def attn_block(self, st, nq, chunks, scale, dst, ready, dacc=False):
    nc = self.nc
    SP, PE, ACT, DVE = self.SP, self.PE, self.ACT, self.DVE
    ob = st.nO % 2
    st.nO += 1
    psO, psD = self.PS[4 + 2 * ob], self.PS[5 + 2 * ob]
    n = len(chunks)
    PE.wait(ready)
    slots = []

    def qk(ci):
        s = st.nS % 3
        st.nS += 1
        slots.append(s)
        PE.wait(st.sfree[s])
        kq = chunks[ci]["kq"]
        for i, (l_, r_) in enumerate(kq):
            ins = nc.tensor.matmul(self.PS[s][:, 0:nq], l_, r_, start=(i == 0), stop=(i == len(kq) - 1))
        return PE.sig(ins)

    tq = [None] * n
    tq[0] = qk(0)
    if n > 1:
        tq[1] = qk(1)
    tpv = None
    for ci in range(n):
        if ci + 2 < n:
            tq[ci + 2] = qk(ci + 2)
        s = slots[ci]
        ACT.wait(tq[ci], st.pfree[s])
        te = ACT.sig(nc.scalar.activation(out=st.pT[s][:, 0:nq], in_=self.PS[s][:, 0:nq], func=AF.Exp, scale=scale))
        st.sfree[s] = te
        e = chunks[ci].get("e")
        if e is not None:
            DVE.wait(te)
            te = DVE.sig(nc.vector.tensor_tensor(out=st.pT[s][:, 0:nq], in0=st.pT[s][:, 0:nq], in1=e, op=ALU.mult))
        PE.wait(te)
        if ci == 0:
            PE.wait(st.ofree[ob])
        if not dacc:
            nc.tensor.matmul(psO[:, 0:nq], chunks[ci]["v"], st.pT[s][:, 0:nq], start=(ci == 0), stop=(ci == n - 1))
            tpv = PE.sig(nc.tensor.matmul(psD[:, 0:nq], self.onesb[:], st.pT[s][:, 0:nq], start=(ci == 0),
                                          stop=(ci == n - 1)))
            st.pfree[s] = tpv
        else:
            tpv = PE.sig(nc.tensor.matmul(psO[:, 0:nq], chunks[ci]["v"], st.pT[s][:, 0:nq], start=(ci == 0),
                                          stop=(ci == n - 1)))
            acc = st.acc[ob]
            DVE.wait(te)
            if ci == 0:
                DVE.wait(st.accfree[ob])
                tacc = DVE.chain(nc.vector.tensor_copy(out=acc[:, 0:nq], in_=st.pT[s][:, 0:nq]))
            else:
                tacc = DVE.chain(nc.vector.tensor_tensor(out=acc[:, 0:nq], in0=acc[:, 0:nq], in1=st.pT[s][:, 0:nq],
                                                         op=ALU.add))
            st.pfree[s] = [tpv, tacc]
    if dacc:
        PE.wait(tacc)
        tpv = PE.sig(nc.tensor.matmul(psD[:, 0:nq], self.onesf[:], st.acc[ob][:, 0:nq], start=True, stop=True))
        st.accfree[ob] = tpv
    DVE.wait(tpv, st.recfree[ob])
    DVE.chain(nc.vector.reciprocal(out=st.rec[ob][:, 0:nq], in_=psD[:, 0:nq]))
    DVE.wait(st.ostore[ob])
    to = DVE.sig(nc.vector.tensor_tensor(out=st.o[ob][:, 0:nq], in0=psO[:, 0:nq], in1=st.rec[ob][:, 0:nq], op=ALU.mult))
    st.ofree[ob] = to
    st.recfree[ob] = to
    SP.wait(to)
    st.ostore[ob] = self.dma(SP, dst, st.o[ob][:, 0:nq], self.ds(f"ao{ob}"))
    return tpv


def mix0_inproj(self):
    nc = self.nc
    SP, PE, ACT, DVE, POOL = self.SP, self.PE, self.ACT, self.DVE, self.POOL
    with contextlib.ExitStack() as es:
        sb = lambda n, s, d: es.enter_context(self._sbuf(n, s, d))
        hT = [sb(f"p_h{i}", [128, DC, TS], BF16) for i in range(2)]
        wst = [sb(f"p_w{i}", [128, 16, 128], BF16) for i in range(3)]
        wv = sb("p_wv", [128, 16, 1024], BF16)
        wk = sb("p_wk", [128, 16, 1024], BF16)
        ob = [sb(f"p_ob{i}", [128, TS], BF16) for i in range(2)]
        of = [sb(f"p_of{i}", [128, TS], F32) for i in range(2)]
        zero = sb("p_zero", [128, 16], F32)
        SP.wait(self.cv["m0"])
        tz = DVE.sig(nc.vector.memset(zero[:], 0.0))
        SP.wait(tz)
        for (p0, n_, t0) in _units():
            if n_ == 256 or t0 == NPR:
                self.dma(SP, self.UT[:, :, p0 - UPAD:p0].rearrange("c p t -> p c t"),
                         zero[:].rearrange("p (c t) -> p c t", t=UPAD)[:, 0:8, :] if False else
                         zero[:, 0:UPAD].unsqueeze(1).to_broadcast([128, 8, UPAD]), self.ds("pz"))
            if n_ == 256 or t0 == NT - TS:
                self.dma(SP, self.UT[:, :, p0 + n_:p0 + n_ + UPAD].rearrange("c p t -> p c t"),
                         zero[:, 0:UPAD].unsqueeze(1).to_broadcast([128, 8, UPAD]), self.ds("pz"))
        self.dma(SP, wv[:], self.WVB, self.ds("pwv"))
        twk = self.dma(SP, wk[:], self.WKB, self.ds("pwv"))
        twv = twk
        hfree = [None, None]
        wfree = [None, None, None]
        psf = [None] * 4
        obf = [None, None]
        off = [None, None]
        n = 0
        nv = 0
        units = _units()
        wtok = {}
        htok = {}

        def issue_w(idx):
            ws = idx % 3
            SP.wait(wfree[ws])
            wtok[idx] = self.dma(SP, wst[ws][:], self.WINB[idx % 24], self.ds(f"pw{ws}"))

        def issue_h(tile):
            sl_ = tile % 2
            SP.wait(hfree[sl_])
            htok[tile] = self.dma(SP, hT[sl_][:], self.HT[:, :, tile * TS:(tile + 1) * TS].rearrange("c p t -> p c t"),
                                  self.ds(f"ph{sl_}"))

        issue_h(0)
        issue_w(0)
        issue_w(1)
        for tile in range(NTILE):
            sl = tile % 2
            tk = slice(tile * TS, (tile + 1) * TS)
            if tile + 1 < NTILE:
                issue_h(tile + 1)
            th = htok.pop(tile)
            PE.wait(th)
            for cc in range(24):
                idx = tile * 24 + cc
                ws = idx % 3
                if idx + 2 < NTILE * 24:
                    issue_w(idx + 2)
                tw = wtok.pop(idx)
                b = n % 2
                n += 1
                ps = self.PS[b]
                PE.wait(tw, psf[b])
                for kc in range(16):
                    ins = nc.tensor.matmul(ps[:], wst[ws][:, kc, :], hT[sl][:, kc, :], start=(kc == 0), stop=(kc == 15))
                tp = PE.sig(ins)
                wfree[ws] = tp
                if cc < 16:
                    ACT.wait(tp, obf[b])
                    te = ACT.sig(nc.scalar.copy(out=ob[b][:], in_=ps[:]))
                    psf[b] = te
                    SP.wait(te)
                    dst = (self.QT if cc < 8 else self.KT)[cc % 8, :, tk]
                    obf[b] = self.dma(SP, dst, ob[b][:], self.ds(f"pob{b}"))
                else:
                    DVE.wait(tp, off[b])
                    te = DVE.sig(nc.vector.tensor_copy(out=of[b][:], in_=ps[:]))
                    psf[b] = te
                    SP.wait(te)
                    c = cc - 16
                    if tile == 0:
                        self.dma(SP, self.UT[c, :, units[0][0]:units[0][0] + 256], of[b][:, 0:256], self.ds(f"pof{b}"))
                        off[b] = self.dma(SP, self.UT[c, :, units[1][0]:units[1][0] + 256], of[b][:, 256:512],
                                          self.ds(f"pof{b}"))
                    else:
                        p0 = units[tile + 1][0]
                        off[b] = self.dma(SP, self.UT[c, :, p0:p0 + TS], of[b][:], self.ds(f"pof{b}"))
            PE.wait(twv, twk)
            for kind in (("v", "k") if tile == 0 else ("v",)):
                wsrc = wv if kind == "v" else wk
                for s in range(4):
                    for half in range(2):
                        b = 2 + nv % 2
                        nv += 1
                        ps = self.PS[b]
                        PE.wait(psf[b])
                        for kc in range(16):
                            ins = nc.tensor.matmul(ps[:], hT[sl][:, kc, s * 128:(s + 1) * 128],
                                                   wsrc[:, kc, half * 512:(half + 1) * 512],
                                                   start=(kc == 0), stop=(kc == 15))
                        tp = PE.sig(ins)
                        r0 = tile * TS + s * 128
                        bb = b - 2
                        if tile == 0:
                            DVE.wait(tp, off[bb])
                            tdv = DVE.sig(nc.vector.tensor_copy(out=of[bb][:], in_=ps[:]))
                            psf[b] = tdv
                            tuse = [tdv]
                            if kind == "v":
                                ACT.wait(tdv, obf[bb])
                                te = ACT.sig(nc.scalar.copy(out=ob[bb][:], in_=of[bb][:]))
                                SP.wait(te)
                                obf[bb] = self.dma(SP, self.VS[r0:r0 + 128, half * 512:(half + 1) * 512], ob[bb][:],
                                                   self.ds(f"pob{bb}"))
                                tuse.append(te)
                            SP.wait(tdv)
                            dsto = self.onv if kind == "v" else self.onk
                            td_ = self.dma(SP, dsto[s * 128:(s + 1) * 128, half * 512:(half + 1) * 512], of[bb][:],
                                           self.ds(f"pof{bb}"))
                            off[bb] = [td_] + tuse
                        else:
                            ACT.wait(tp, obf[bb])
                            te = ACT.sig(nc.scalar.copy(out=ob[bb][:], in_=ps[:]))
                            psf[b] = te
                            SP.wait(te)
                            obf[bb] = self.dma(SP, self.VS[r0:r0 + 128, half * 512:(half + 1) * 512], ob[bb][:],
                                               self.ds(f"pob{bb}"))
            hfree[sl] = tp
        self.barrier()


def mix0_pool(self):
    nc = self.nc
    SP, PE, ACT, DVE, POOL = self.SP, self.PE, self.ACT, self.DVE, self.POOL
    with contextlib.ExitStack() as es:
        sb = lambda n, s, d: es.enter_context(self._sbuf(n, s, d))
        pw = sb("q_pw", [128, 8, 256], BF16)
        u = [sb(f"q_u{i}", [128, 8, TS + 16], F32) for i in range(2)]
        inv = [sb(f"q_inv{i}", [128, 4, TS], F32) for i in range(2)]
        t1 = sb("q_t1", [128, TS + 16], F32)
        t2 = sb("q_t2", [128, TS + 16], F32)
        dT = [sb(f"q_d{i}", [128, 8, TS], BF16) for i in range(2)]
        ob = [sb(f"q_ob{i}", [128, TS], BF16) for i in range(2)]
        tpw = self.dma(POOL, pw[:], self.pool_w[0].rearrange("g (cc p) d -> p (g cc) d", p=128), self.ds("sw0"))
        ufree = [None, None]
        dfree = [None, None]
        psf = [None, None]
        obf = [None, None]
        n = 0
        for ui, (p0, nn, t0) in enumerate(_units()):
            sl = ui % 2
            SP.wait(ufree[sl])
            self.dma(SP, u[sl][:, :, 0:nn + 16], self.UT[:, :, p0 - UPAD:p0 + nn + UPAD].rearrange("c p t -> p c t"),
                     self.ds(f"qu{sl}"))
            tl = self.dma(SP, inv[sl][:, :, 0:nn], self.invcnt[:, :, p0:p0 + nn].rearrange("g p t -> p g t"),
                          self.ds(f"qu{sl}"))
            DVE.wait(tl, dfree[sl])
            td = None
            for c in range(8):
                g = c // 2
                w = (2, 4, 8, 16)[g]
                uu = u[sl][:, c, :]
                L = nn + 15
                DVE.chain(nc.vector.tensor_tensor(out=t1[:, 0:L], in0=uu[:, 0:L], in1=uu[:, 1:L + 1], op=ALU.add))
                cur, oth = t1, t2
                step = 2
                while step < w:
                    L2 = L - step
                    DVE.chain(nc.vector.tensor_tensor(out=oth[:, 0:L2], in0=cur[:, 0:L2], in1=cur[:, step:step + L2],
                                                      op=ALU.add))
                    cur, oth = oth, cur
                    L = L2
                    step *= 2
                o0 = UPAD - w // 2
                DVE.chain(nc.vector.tensor_tensor(out=oth[:, 0:nn], in0=cur[:, o0:o0 + nn], in1=inv[sl][:, g, 0:nn],
                                                  op=ALU.mult))
                td = DVE.chain(nc.vector.tensor_tensor(out=dT[sl][:, c, 0:nn], in0=oth[:, 0:nn], in1=uu[:, UPAD:UPAD + nn],
                                                       op=ALU.subtract))
            ufree[sl] = td
            PE.wait(td, tpw)
            tp = None
            for g in range(4):
                for dch in range(2):
                    b = n % 2
                    n += 1
                    ps = self.PS[b]
                    PE.wait(psf[b])
                    for cc in range(2):
                        ins = nc.tensor.matmul(ps[:, 0:nn], pw[:, g * 2 + cc, dch * 128:(dch + 1) * 128],
                                               dT[sl][:, g * 2 + cc, 0:nn], start=(cc == 0), stop=(cc == 1))
                    tp = PE.sig(ins)
                    ACT.wait(tp, obf[b])
                    oc = g * 2 + dch
                    te = ACT.sig(nc.scalar.activation(out=ob[b][:, 0:nn], in_=ps[:, 0:nn], func=AF.Identity,
                                                      scale=self.PSC[:, oc:oc + 1]))
                    psf[b] = te
                    SP.wait(te)
                    obf[b] = self.dma(SP, self.CT[8 + oc, :, t0:t0 + nn], ob[b][:, 0:nn], self.ds(f"qob{b}"))
            dfree[sl] = tp
        self.barrier()


def mix0_attn_prompt(self):
    nc = self.nc
    SP, PE = self.SP, self.PE
    with contextlib.ExitStack() as es:
        sb = lambda n, s, d: es.enter_context(self._sbuf(n, s, d))
        st = attn_setup(self, es, TS)
        qa = [sb(f"r_q{i}", [128, 8, 256], BF16) for i in range(2)]
        ka = [sb(f"r_k{i}", [128, 8, 256], BF16) for i in range(2)]
        va = [sb(f"r_v{i}", [128, 2, 1024], BF16) for i in range(2)]
        for s in range(2):
            tk = slice(s * 256, (s + 1) * 256)
            self.dma(SP, qa[s][:], self.QT[:, :, tk].rearrange("h p t -> p h t"), self.ds(f"rl{s}"))
            self.dma(SP, ka[s][:], self.KT[:, :, tk].rearrange("h p t -> p h t"), self.ds(f"rl{s}"))
            tl = self.dma(SP, va[s][:], self.VS[s * 256:(s + 1) * 256, :].rearrange("(kc p) d -> p kc d", p=128),
                          self.ds(f"rl{s}"))
            for h in range(8):
                chunks = [dict(kq=[(ka[s][:, h, kc * 128:(kc + 1) * 128], qa[s][:, h, :])],
                               v=va[s][:, kc, h * 128:(h + 1) * 128]) for kc in range(2)]
                attn_block(self, st, 256, chunks, 128 ** -0.5, self.CT[h, :, tk], tl)
        self.barrier()


E_VALID = {0: list(range(0, 6)), 1: list(range(0, 8)), 2: list(range(2, 8))}
E_ABASE = {0: 15, 1: 19, 2: 23}


def mix0_ebuild(self):
    with contextlib.ExitStack() as es:
        self.ebuild_body(es)
        self.barrier()


def ebuild_body(self, es):
    nc = self.nc
    SP, PE, ACT, DVE = self.SP, self.PE, self.ACT, self.DVE
    if True:
        sb = lambda n, s, d: es.enter_context(self._sbuf(n, s, d))
        raw = [sb(f"b_raw{i}", [128, TS], F32) for i in range(2)]
        ex = [sb(f"b_ex{i}", [128, TS], F32) for i in range(2)]
        mk = [sb(f"b_mk{i}", [128, TS], F32) for i in range(2)]
        eo = [sb(f"b_eo{i}", [128, TS], BF16) for i in range(2)]
        rawf = [None, None]
        exf = [None, None]
        eof = [None, None]
        mkf = [None, None]
        n = 0
        m = 0
        rt = self.rpbrep.tensor
        for var in range(3):
            for k in E_VALID[var]:
                ms = m % 2
                m += 1
                SP.wait(mkf[ms])
                tm = self.dma(SP, mk[ms][:], self.emask[var, k], self.ds(f"bm{ms}"))
                td = None
                for h in range(8):
                    b = n % 2
                    n += 1
                    SP.wait(rawf[b])
                    for t in range(2):
                        a0 = E_ABASE[var] - 2 * k - t
                        off = ((h * 31 + a0) * 64) * 127 + 63
                        src = bass.AP(tensor=rt, offset=off, ap=[[126, 64], [64 * 127, 8], [1, 64]])
                        tr = self.dma(SP, raw[b][t * 64:(t + 1) * 64, :].rearrange("p (m c) -> p m c", c=64), src,
                                      self.ds(f"br{b}"))
                    ACT.wait(tr, exf[b])
                    te = ACT.sig(nc.scalar.activation(out=ex[b][:], in_=raw[b][:], func=AF.Exp))
                    rawf[b] = te
                    DVE.wait(te, tm, eof[b])
                    td = DVE.sig(nc.vector.tensor_tensor(out=eo[b][:], in0=ex[b][:], in1=mk[ms][:], op=ALU.mult))
                    exf[b] = td
                    SP.wait(td)
                    eof[b] = self.dma(SP, self.EALL[var, h, k], eo[b][:], self.ds(f"be{b}"))
                mkf[ms] = td


def mix0_attn_sample(self):
    nc = self.nc
    SP, PE, POOL = self.SP, self.PE, self.POOL
    with contextlib.ExitStack() as es:
        sb = lambda n, s, d: es.enter_context(self._sbuf(n, s, d))
        st = attn_setup(self, es, TS)
        qa = [sb(f"s_q{i}", [128, 8, TS], BF16) for i in range(2)]
        ka = [sb(f"s_k{i}", [128, 8, 1024], BF16) for i in range(2)]
        va = [sb(f"s_v{i}", [128, 8, 1024], BF16) for i in range(2)]
        ea = [sb(f"s_e{i}", [128, 8, TS], BF16) for i in range(2)]
        kc_ = sb("s_kc", [128, 8, 256], BF16)
        vc = sb("s_vc", [128, 2, 1024], BF16)
        tc1 = self.dma(SP, kc_[:], self.KCT.rearrange("h p t -> p h t"), self.ds("sc"))
        tc2 = self.dma(POOL, vc[:], self.cnav.rearrange("(kc p) d -> p kc d", p=128), self.ds("sw0"))
        PE.wait(tc1, tc2)
        blkfree = [None, None]
        efree = [None, None]
        ne = 0
        for b in range(8):
            var = 0 if b == 0 else (2 if b == 7 else 1)
            kr0 = min(max(8 * b - 4, 0), 48)
            sl = b % 2
            q0 = NPR + b * TS
            k0 = NPR + kr0 * 64
            SP.wait(blkfree[sl])
            self.dma(SP, qa[sl][:], self.QT[:, :, q0:q0 + TS].rearrange("h p t -> p h t"), self.ds(f"sl{sl}"))
            self.dma(SP, ka[sl][:], self.KT[:, :, k0:k0 + 1024].rearrange("h p t -> p h t"), self.ds(f"sl{sl}"))
            tl = self.dma(SP, va[sl][:], self.VS[k0:k0 + 1024, :].rearrange("(kc p) d -> p kc d", p=128),
                          self.ds(f"sl{sl}"))
            tlast = None
            for h in range(8):
                es_ = ne % 2
                ne += 1
                SP.wait(efree[es_])
                te = self.dma(SP, ea[es_][:], self.EALL[var, h].rearrange("k p q -> p k q"), self.ds(f"se{es_}"))
                chunks = []
                for k in E_VALID[var]:
                    chunks.append(dict(kq=[(ka[sl][:, h, k * 128:(k + 1) * 128], qa[sl][:, h, :])],
                                       v=va[sl][:, k, h * 128:(h + 1) * 128], e=ea[es_][:, k, :]))
                for kc in range(2):
                    chunks.append(dict(kq=[(kc_[:, h, kc * 128:(kc + 1) * 128], qa[sl][:, h, :])],
                                       v=vc[:, kc, h * 128:(h + 1) * 128]))
                self.DVE.wait(te)
                tlast = attn_block(self, st, TS, chunks, 128 ** -0.5, self.CT[h, :, q0:q0 + TS], [tl, te])
                efree[es_] = [tlast, (self.DVE, self.DVE.cnt)]
            blkfree[sl] = tlast
        self.barrier()


KB.ctx_prep = ctx_prep
KB.mix0_inproj = mix0_inproj
KB.mix0_pool = mix0_pool
KB.mix0_attn_prompt = mix0_attn_prompt
KB.mix0_ebuild = mix0_ebuild
KB.ebuild_body = ebuild_body
KB.mix0_attn_sample = mix0_attn_sample


def fakemod(self):
    nc = self.nc
    for t in (self.SC1, self.SH1):
        nc.vector.memset(t[:], 0.5)
    for t in (self.GSC, self.G2, self.B2):
        nc.vector.memset(t[:], 0.25)
    self.DVE.sig(nc.vector.memset(self.MOD[:], 0.1))
    self.barrier()


KB.fakemod = fakemod


def rms_fm(self, buf, psS, gvec, dst_fn, ob, obf, tk, tokbuf, sqt, sqf, rt):
    nc = self.nc
    SP, PE, ACT, DVE = self.SP, self.PE, self.ACT, self.DVE
    tS = None
    for c4 in range(4):
        b = c4 % 2
        ACT.wait(tokbuf[c4], sqf[b])
        tq = ACT.sig(nc.scalar.activation(out=sqt[b][:], in_=buf[:, c4, :], func=AF.Square))
        PE.wait(tq)
        if c4 == 0:
            PE.wait(rt.get("psfree"))
        tS = PE.sig(nc.tensor.matmul(psS[:], self.onesf[:], sqt[b][:], start=(c4 == 0), stop=(c4 == 3)))
        sqf[b] = tS
    DVE.wait(tS)
    tv = DVE.chain(nc.vector.tensor_scalar(out=rt["ms"][:], in0=psS[:], scalar1=1.0 / 512, scalar2=RMS_EPS,
                                           op0=ALU.mult, op1=ALU.add))
    rt["psfree"] = tv
    ACT.wait(tv)
    tsd = ACT.sig(nc.scalar.activation(out=rt["rstd"][:], in_=rt["ms"][:], func=AF.Sqrt))
    DVE.wait(tsd)
    DVE.chain(nc.vector.reciprocal(out=rt["rstd"][:], in_=rt["rstd"][:]))
    for c4 in range(4):
        b = c4 % 2
        DVE.wait(obf[b])
        to = DVE.sig(nc.vector.scalar_tensor_tensor(out=ob[b][:], in0=buf[:, c4, :], scalar=gvec[:, c4:c4 + 1],
                                                    in1=rt["rstd"][:], op0=ALU.mult, op1=ALU.mult))
        SP.wait(to)
        obf[b] = self.dma(SP, dst_fn(c4), ob[b][:], self.ds(f"dob{b}"))
    return to


def rope_evac(self, psA, psB, cs, sn, t1, t2, out_ap, n):
    nc = self.nc
    DVE = self.DVE
    DVE.chain(nc.vector.tensor_tensor(out=t1[0:64, 0:n], in0=psA[0:64, 0:n], in1=cs[0:64, 0:n], op=ALU.mult))
    DVE.chain(nc.vector.tensor_tensor(out=t2[0:64, 0:n], in0=psB[0:64, 0:n], in1=sn[0:64, 0:n], op=ALU.mult))
    return DVE.chain(nc.vector.tensor_tensor(out=out_ap, in0=t1[0:64, 0:n], in1=t2[0:64, 0:n], op=ALU.add))


def mla_down(self):
    nc = self.nc
    SP, PE, ACT, DVE, POOL = self.SP, self.PE, self.ACT, self.DVE, self.POOL
    with contextlib.ExitStack() as es:
        sb = lambda n, s, d: es.enter_context(self._sbuf(n, s, d))
        hT = [sb(f"d_h{i}", [128, DC, TS], BF16) for i in range(2)]
        wdn = sb("d_w", [128, 16, 1152], BF16)
        bufq = sb("d_bq", [128, 4, TS], F32)
        bufk = sb("d_bk", [128, 4, TS], F32)
        sqt = [sb(f"d_sq{i}", [128, TS], F32) for i in range(2)]
        rt = dict(ms=sb("d_ms", [128, TS], F32), rstd=sb("d_rs", [128, TS], F32))
        ob = [sb(f"d_ob{i}", [128, TS], BF16) for i in range(2)]
        cs = [sb(f"d_cs{i}", [64, TS], F32) for i in range(2)]
        sn = [sb(f"d_sn{i}", [64, TS], F32) for i in range(2)]
        t1 = sb("d_t1", [64, TS], F32)
        t2 = sb("d_t2", [64, TS], F32)
        ko = [sb(f"d_ko{i}", [64, TS], BF16) for i in range(2)]
        kvrow = sb("d_kvrow", [1, 512], F32)
        kvbc = sb("d_kvbc", [128, 512], F32)
        ta = sb("d_ta", [128, 512], F32)
        tb_ = sb("d_tb", [128, 64], F32)
        ss = sb("d_ss", [128, 2], F32)
        junk = sb("d_junk", [128, 512], F32)
        dsw = self.ds("sw0")
        src = self.mla_w_down[0]
        self.dma(POOL, wdn[:, :, 0:1088], src.rearrange("(kc p) n -> p kc n", p=128), dsw)
        pe_src = src[:, 1024:1088].rearrange("(kc p) (b2 b1 e) -> p kc b2 b1 e", p=128, b2=2, b1=2)
        pe_dst = wdn[:, :, 1088:1152].rearrange("p kc (b2 b1 e) -> p kc b2 b1 e", b2=2, b1=2)
        tw = None
        for b2 in range(2):
            for b1 in range(2):
                tw = self.dma(POOL, pe_dst[:, :, b2, 1 - b1, :], pe_src[:, :, b2, b1, :], dsw)
        tkr = self.dma(SP, kvrow[:], self.mla_kv_norm, self.ds("dkr"))
        PE.wait(tw, tkr)
        tp = PE.sig(nc.tensor.matmul(self.PS[7][:], self.onesf[0:1, :], kvrow[0:1, :], start=True, stop=True))
        DVE.wait(tp)
        tkb = DVE.chain(nc.vector.tensor_copy(out=kvbc[:], in_=self.PS[7][:]))
        PE.wait(tkb)
        hfree = [None, None]
        psf = [None] * 8
        obf = [None, None]
        sqf = [None, None]
        kof = [None, None]
        csf = [None, None]
        n = 0
        for tile in range(NTILE):
            sl = tile % 2
            tk = slice(tile * TS, (tile + 1) * TS)
            SP.wait(hfree[sl])
            th = self.dma(SP, hT[sl][:], self.HT[:, :, tk].rearrange("c p t -> p c t"), self.ds(f"dh{sl}"))
            tcs = None
            if tile > 0:
                SP.wait(csf[sl])
                self.dma(SP, cs[sl][:], self.ropec[:, (tile - 1) * TS:tile * TS], self.ds(f"dcs{sl}"))
                tcs = self.dma(SP, sn[sl][:], self.ropes[:, (tile - 1) * TS:tile * TS], self.ds(f"dcs{sl}"))
            PE.wait(th)
            for gi, (buf, col0, gvec, dstT, psS) in enumerate(((bufq, 0, self.QNG, self.CQT, self.PS[2]),
                                                              (bufk, 512, self.KVG, self.CKVT, self.PS[3]))):
                tokbuf = [None] * 4
                for c4 in range(4):
                    b = n % 2
                    n += 1
                    ps = self.PS[b]
                    PE.wait(psf[b])
                    for kc in range(16):
                        ins = nc.tensor.matmul(ps[:], wdn[:, kc, col0 + c4 * 128:col0 + (c4 + 1) * 128], hT[sl][:, kc, :],
                                               start=(kc == 0), stop=(kc == 15))
                    tpp = PE.sig(ins)
                    DVE.wait(tpp)
                    tokbuf[c4] = DVE.sig(nc.vector.tensor_copy(out=buf[:, c4, :], in_=ps[:]))
                    psf[b] = tokbuf[c4]
                rms_fm(self, buf, psS, gvec, lambda c4, dstT=dstT: dstT[c4, :, tk], ob, obf, tk, tokbuf, sqt, sqf, rt)
            kb = tile % 2
            PE.wait(psf[4], psf[5])
            for kc in range(16):
                ins = nc.tensor.matmul(self.PS[4][0:64, :], wdn[:, kc, 1024:1088], hT[sl][:, kc, :],
                                       start=(kc == 0), stop=(kc == 15))
            tA = PE.sig(ins)
            if tile > 0:
                for kc in range(16):
                    ins = nc.tensor.matmul(self.PS[5][0:64, :], wdn[:, kc, 1088:1152], hT[sl][:, kc, :],
                                           start=(kc == 0), stop=(kc == 15))
                tB = PE.sig(ins)
                DVE.wait(tA, tB, tcs, kof[kb])
                tko = rope_evac(self, self.PS[4], self.PS[5], cs[sl], sn[sl], t1, t2, ko[kb][0:64, :], TS)
                csf[sl] = tko
            else:
                DVE.wait(tA, kof[kb])
                tko = DVE.sig(nc.vector.tensor_copy(out=ko[kb][:], in_=self.PS[4][0:64, :]))
            psf[4] = tko
            psf[5] = tko
            SP.wait(tko)
            kof[kb] = self.dma(SP, self.KPET[:, tk], ko[kb][:], self.ds(f"dko{kb}"))
            tlast = tA
            if tile == 0:
                for s in range(4):
                    PE.wait(psf[6], psf[7])
                    for kc in range(16):
                        ins = nc.tensor.matmul(self.PS[6][:], hT[sl][:, kc, s * 128:(s + 1) * 128], wdn[:, kc, 512:1024],
                                               start=(kc == 0), stop=(kc == 15))
                    tA2 = PE.sig(ins)
                    for kc in range(16):
                        ins = nc.tensor.matmul(self.PS[7][:, 0:64], hT[sl][:, kc, s * 128:(s + 1) * 128],
                                               wdn[:, kc, 1024:1088], start=(kc == 0), stop=(kc == 15))
                    tB2 = PE.sig(ins)
                    tlast = tB2
                    ACT.wait(tA2)
                    tsq = ACT.sig(nc.scalar.activation(out=junk[:], in_=self.PS[6][:], func=AF.Square,
                                                       accum_out=ss[:, 0:1]))
                    DVE.wait(tsq)
                    tv = DVE.chain(nc.vector.tensor_scalar(out=ss[:, 1:2], in0=ss[:, 0:1], scalar1=1.0 / 512,
                                                           scalar2=RMS_EPS, op0=ALU.mult, op1=ALU.add))
                    ACT.wait(tv)
                    tsd = ACT.sig(nc.scalar.activation(out=ss[:, 0:1], in_=ss[:, 1:2], func=AF.Sqrt))
                    DVE.wait(tsd)
                    DVE.chain(nc.vector.reciprocal(out=ss[:, 1:2], in_=ss[:, 0:1]))
                    DVE.wait((self.ds("dta"), self.ds("dta").cnt))
                    tta = DVE.sig(nc.vector.scalar_tensor_tensor(out=ta[:], in0=self.PS[6][:], scalar=ss[:, 1:2],
                                                                 in1=kvbc[:], op0=ALU.mult, op1=ALU.mult))
                    psf[6] = tta
                    DVE.wait(tB2)
                    ttb = DVE.sig(nc.vector.tensor_copy(out=tb_[:], in_=self.PS[7][:, 0:64]))
                    psf[7] = ttb
                    SP.wait(tta, ttb)
                    self.dma(SP, self.onckv[s * 128:(s + 1) * 128, :], ta[:], self.ds("dta"))
                    self.dma(SP, self.onkpe[s * 128:(s + 1) * 128, :], tb_[:], self.ds("dta"))
            hfree[sl] = tlast
        self.barrier()


def mla_qproj(self):
    nc = self.nc
    SP, PE, ACT, DVE, POOL = self.SP, self.PE, self.ACT, self.DVE, self.POOL
    with contextlib.ExitStack() as es:
        sb = lambda n, s, d: es.enter_context(self._sbuf(n, s, d))
        wuq = sb("u_w", [128, 4, 3072 + 1024], BF16)
        cq = [sb(f"u_cq{i}", [128, 4, TS], BF16) for i in range(2)]
        cs = [sb(f"u_cs{i}", [64, TS], F32) for i in range(2)]
        sn = [sb(f"u_sn{i}", [64, TS], F32) for i in range(2)]
        t1 = sb("u_t1", [64, TS], F32)
        t2 = sb("u_t2", [64, TS], F32)
        ob = [sb(f"u_ob{i}", [128, TS], BF16) for i in range(2)]
        po = [sb(f"u_po{i}", [64, TS], BF16) for i in range(2)]
        dsw = self.ds("sw0")
        src = self.mla_w_uq[0]
        self.dma(POOL, wuq[:, :, 0:3072], src.rearrange("(kc p) n -> p kc n", p=128), dsw)
        tw = None
        for kc in range(4):
            sv = src[kc * 128:(kc + 1) * 128, :].rearrange("p (h x) -> p h x", x=192)[:, :, 128:192]
            sv = sv.rearrange("p h (b2 b1 e) -> p h b2 b1 e", b2=2, b1=2)
            dv = wuq[:, kc, 3072:4096].rearrange("p (h b2 b1 e) -> p h b2 b1 e", h=16, b2=2, b1=2)
            for b2 in range(2):
                for b1 in range(2):
                    tw = self.dma(POOL, dv[:, :, b2, 1 - b1, :], sv[:, :, b2, b1, :], dsw)
        PE.wait(tw)
        cfree = [None, None]
        csf = [None, None]
        psf = [None] * 6
        obf = [None, None]
        pof = [None, None]
        n = 0
        m = 0
        for tile in range(NTILE):
            sl = tile % 2
            tk = slice(tile * TS, (tile + 1) * TS)
            SP.wait(cfree[sl])
            tc_ = self.dma(SP, cq[sl][:], self.CQT[:, :, tk].rearrange("c p t -> p c t"), self.ds(f"uc{sl}"))
            tcs = None
            if tile > 0:
                SP.wait(csf[sl])
                self.dma(SP, cs[sl][:], self.ropec[:, (tile - 1) * TS:tile * TS], self.ds(f"ucs{sl}"))
                tcs = self.dma(SP, sn[sl][:], self.ropes[:, (tile - 1) * TS:tile * TS], self.ds(f"ucs{sl}"))
            PE.wait(tc_)
            tko = None
            for h in range(16):
                b = n % 2
                n += 1
                ps = self.PS[b]
                PE.wait(psf[b])
                for c4 in range(4):
                    ins = nc.tensor.matmul(ps[:], wuq[:, c4, h * 192:h * 192 + 128], cq[sl][:, c4, :],
                                           start=(c4 == 0), stop=(c4 == 3))
                tp = PE.sig(ins)
                ACT.wait(tp, obf[b])
                te = ACT.sig(nc.scalar.copy(out=ob[b][:], in_=ps[:]))
                psf[b] = te
                SP.wait(te)
                obf[b] = self.dma(SP, self.QNT[h, :, tk], ob[b][:], self.ds(f"uob{b}"))
                pb = m % 2
                m += 1
                psA, psB = self.PS[2 + 2 * pb], self.PS[3 + 2 * pb]
                PE.wait(psf[2 + 2 * pb], psf[3 + 2 * pb])
                for c4 in range(4):
                    ins = nc.tensor.matmul(psA[0:64, :], wuq[:, c4, h * 192 + 128:h * 192 + 192], cq[sl][:, c4, :],
                                           start=(c4 == 0), stop=(c4 == 3))
                tA = PE.sig(ins)
                if tile > 0:
                    for c4 in range(4):
                        ins = nc.tensor.matmul(psB[0:64, :], wuq[:, c4, 3072 + h * 64:3072 + (h + 1) * 64], cq[sl][:, c4, :],
                                               start=(c4 == 0), stop=(c4 == 3))
                    tB = PE.sig(ins)
                    DVE.wait(tA, tB, tcs, pof[pb])
                    tko = rope_evac(self, psA, psB, cs[sl], sn[sl], t1, t2, po[pb][0:64, :], TS)
                else:
                    DVE.wait(tA, pof[pb])
                    tko = DVE.sig(nc.vector.tensor_copy(out=po[pb][:], in_=psA[0:64, :]))
                psf[2 + 2 * pb] = tko
                psf[3 + 2 * pb] = tko
                SP.wait(tko)
                pof[pb] = self.dma(SP, self.QPET[h, :, tk], po[pb][:], self.ds(f"upo{pb}"))
            cfree[sl] = (PE, PE.cnt)
            csf[sl] = tko
        self.barrier()


def mla_attn(self):
    nc = self.nc
    SP, PE, ACT, DVE, POOL = self.SP, self.PE, self.ACT, self.DVE, self.POOL
    scale = 192 ** -0.5
    NK = NSA + 256
    NCH = NK // 128
    with contextlib.ExitStack() as es:
        sb = lambda n, s, d: es.enter_context(self._sbuf(n, s, d))
        st = attn_setup(self, es, TS)
        wukv = sb("v_w", [128, 4, 4096], BF16)
        ckvT = sb("v_ckv", [128, 4, NK], BF16)
        kpeT = sb("v_kpe", [64, NK], BF16)
        kn = sb("v_kn", [128, NK], BF16)
        vh = sb("v_vh", [128, NCH, 128], BF16)
        qn = [sb(f"v_qn{i}", [128, NSA], BF16) for i in range(2)]
        qp = [sb(f"v_qp{i}", [64, NSA], BF16) for i in range(2)]
        tw = self.dma(POOL, wukv[:], self.mla_w_ukv[0].rearrange("(kc p) n -> p kc n", p=128), self.ds("sw0"))
        psf = [None] * 4
        PE.wait(tw)

        def project(h, ckv, nkeys, kn_t, vh_t, ready):
            PE.wait(ready)
            t = None
            for k0 in range(0, nkeys, 512):
                w = min(512, nkeys - k0)
                PE.wait(psf[3])
                for c4 in range(4):
                    ins = nc.tensor.matmul(self.PS[3][:, 0:w], wukv[:, c4, h * 256:h * 256 + 128], ckv[:, c4, k0:k0 + w],
                                           start=(c4 == 0), stop=(c4 == 3))
                tp = PE.sig(ins)
                ACT.wait(tp)
                t = ACT.sig(nc.scalar.copy(out=kn_t[:, k0:k0 + w], in_=self.PS[3][:, 0:w]))
                psf[3] = t
            nch = nkeys // 128
            for g0 in range(0, nch, 4):
                gn = min(4, nch - g0)
                PE.wait(psf[3])
                for ci in range(gn):
                    ch = g0 + ci
                    for c4 in range(4):
                        ins = nc.tensor.matmul(self.PS[3][:, ci * 128:(ci + 1) * 128], ckv[:, c4, ch * 128:(ch + 1) * 128],
                                               wukv[:, c4, h * 256 + 128:h * 256 + 256], start=(c4 == 0), stop=(c4 == 3))
                tp = PE.sig(ins)
                ACT.wait(tp)
                t2_ = ACT.sig(nc.scalar.copy(out=vh_t[:, g0:g0 + gn, :].rearrange("p g d -> p (g d)"),
                                             in_=self.PS[3][:, 0:gn * 128]))
                psf[3] = t2_
                t = t2_
            return t

        self.dma(SP, ckvT[:], self.CKVT[:, :, NPR:NT + 256].rearrange("c p t -> p c t"), self.ds("vl"))
        tl = self.dma(SP, kpeT[:], self.KPET[:, NPR:NT + 256], self.ds("vl"))
        qfree = [None, None]
        for h in range(16):
            sl = h % 2
            SP.wait(qfree[sl])
            self.dma(SP, qn[sl][:], self.QNT[h, :, NPR:NT], self.ds(f"vq{sl}"))
            tq = self.dma(SP, qp[sl][:], self.QPET[h, :, NPR:NT], self.ds(f"vq{sl}"))
            tproj = project(h, ckvT, NK, kn, vh, tl)
            tlast = None
            for qb in range(NSA // TS):
                qs = slice(qb * TS, (qb + 1) * TS)
                chunks = [dict(kq=[(kn[:, c * 128:(c + 1) * 128], qn[sl][:, qs]),
                                   (kpeT[0:64, c * 128:(c + 1) * 128], qp[sl][0:64, qs])],
                               v=vh[:, c, :]) for c in range(NCH)]
                tlast = attn_block(self, st, TS, chunks, scale, self.CT[h, :, NPR + qb * TS:NPR + (qb + 1) * TS],
                                   [tq, tproj], dacc=True)
            qfree[sl] = tlast
        ckp = sb("v_ckp", [128, 4, 256], BF16)
        kpp = sb("v_kpp", [64, 256], BF16)
        qna = sb("v_qna", [128, 16, 256], BF16)
        qpa = sb("v_qpa", [64, 16, 256], BF16)
        knp = sb("v_knp", [128, 256], BF16)
        vhp = sb("v_vhp", [128, 2, 128], BF16)
        tprev = tlast
        for s in range(2):
            tk = slice(s * 256, (s + 1) * 256)
            SP.wait(tprev)
            self.dma(SP, ckp[:], self.CKVT[:, :, tk].rearrange("c p t -> p c t"), self.ds("vp"))
            self.dma(SP, kpp[:], self.KPET[:, tk], self.ds("vp"))
            self.dma(SP, qna[:], self.QNT[:, :, tk].rearrange("h p t -> p h t"), self.ds("vp"))
            tlp = self.dma(SP, qpa[:], self.QPET[:, :, tk].rearrange("h p t -> p h t"), self.ds("vp"))
            for h in range(16):
                tproj = project(h, ckp, 256, knp, vhp, tlp)
                chunks = [dict(kq=[(knp[:, c * 128:(c + 1) * 128], qna[:, h, :]),
                                   (kpp[0:64, c * 128:(c + 1) * 128], qpa[0:64, h, :])],
                               v=vhp[:, c, :]) for c in range(2)]
                tprev = attn_block(self, st, 256, chunks, scale, self.CT[h, :, tk], [tlp, tproj])
        self.barrier()


KB.mla_down = mla_down
KB.mla_qproj = mla_qproj
KB.mla_attn = mla_attn
```
